# Optimizing a Trainium2 kernel written in Bass

```python
import math
import jax, jax.numpy as jnp
from jax import lax
import numpy as np

D_MODEL = 2048
BATCH = 4
SEQ = 4096
DEPTH = 2

D_BR = D_MODEL // 2
N_BRANCHES = 4
GMLP_CHUNK = 128
GMLP_GROUPS = 8
GMLP_GROUP_DIM = D_BR // GMLP_GROUPS
LRU_BLOCKS = 8
LRU_BLOCK_DIM = D_BR // LRU_BLOCKS
CONV_WIDTH = 4
LRU_C = 8.0
NSA_HEAD_DIM = 64
NSA_HEADS = D_BR // NSA_HEAD_DIM
NSA_KV_HEADS = NSA_HEADS // 4
NSA_GROUP = NSA_HEADS // NSA_KV_HEADS
NSA_KV_W = NSA_KV_HEADS * NSA_HEAD_DIM
CMP_BLOCK = 32
CMP_STRIDE = 16
CMP_HIDDEN = 256
SLC_BLOCK = 64
SLC_TOPK = 8
WINDOW = 256
Q_BLOCK = 128
MEM_LEN = 256
MEM_HEADS = 4
MEM_HEAD_DIM = D_BR // MEM_HEADS
REL_BUCKETS = 32
REL_MAX_DIST = 1024
DN_ALPHA = (2 * DEPTH) ** 0.25
DN_BETA = (8 * DEPTH) ** -0.25
LN_EPS = 1e-5
IN_WIDTH = 9 * D_BR + 6 * NSA_KV_W + 3 * NSA_HEADS + N_BRANCHES * D_MODEL

kernel_name = "hybrid_gated_gmlp_rglru_nsa_mem_deepnorm"


def in_split_points():
    sizes = (D_BR, D_BR, D_BR, D_BR, D_BR, D_BR,
             NSA_KV_W, NSA_KV_W, NSA_KV_W, NSA_KV_W, NSA_KV_W, NSA_KV_W,
             3 * NSA_HEADS, D_BR, D_BR, D_BR, N_BRANCHES * D_MODEL)
    return tuple(int(v) for v in np.cumsum(sizes)[:-1])


def layer_norm(x, g, b):
    xf = x.astype(jnp.float32)
    mu = jnp.mean(xf, -1, keepdims=True)
    var = jnp.mean(jnp.square(xf - mu), -1, keepdims=True)
    return ((xf - mu) * lax.rsqrt(var + LN_EPS) * g + b).astype(x.dtype)


def masked_softmax(s, mask):
    s = jnp.where(mask, s.astype(jnp.float32), -1e30)
    m = jnp.max(s, axis=-1, keepdims=True)
    p = jnp.where(mask, jnp.exp(s - m), 0.0)
    return p / jnp.maximum(jnp.sum(p, -1, keepdims=True), 1e-30)


def rel_bucket(dist):
    n = jnp.maximum(dist, 0)
    exact = REL_BUCKETS // 2
    nf = jnp.maximum(n, 1).astype(jnp.float32)
    large = exact + (jnp.log(nf / exact) / math.log(REL_MAX_DIST / exact)
                     * (REL_BUCKETS - exact)).astype(jnp.int32)
    return jnp.where(n < exact, n, jnp.minimum(large, REL_BUCKETS - 1))


def gmlp_spatial_gating(u, v, ln_g, ln_b, w_s, b_s):
    bsz, seq, _ = u.shape
    u = jax.nn.gelu(u)
    v = layer_norm(jax.nn.gelu(v), ln_g, ln_b)
    vc = v.reshape(bsz, seq // GMLP_CHUNK, GMLP_CHUNK, GMLP_GROUPS, GMLP_GROUP_DIM)
    causal = jnp.tril(jnp.ones((GMLP_CHUNK, GMLP_CHUNK), dtype=bool))
    w = jnp.where(causal, w_s, 0).astype(v.dtype)
    mixed = jnp.einsum('gts,bcsgd->bctgd', w, vc) + jnp.swapaxes(b_s, 0, 1)[:, :, None]
    return u * mixed.reshape(bsz, seq, D_BR)


def causal_depthwise_conv(x, w, b):
    out = lax.conv_general_dilated(
        x, w[:, None, :].astype(x.dtype), window_strides=(1,),
        padding=[(CONV_WIDTH - 1, 0)], dimension_numbers=('NWC', 'WIO', 'NWC'),
        feature_group_count=x.shape[-1])
    return out + b


def block_diag_linear(x, w, b):
    bsz, seq, _ = x.shape
    xb = x.reshape(bsz, seq, LRU_BLOCKS, LRU_BLOCK_DIM)
    return jnp.einsum('bsnd,nde->bsne', xb, w).reshape(bsz, seq, D_BR) + b


def rg_lru(x, wa, ba, wx, bx, lam):
    f32 = jnp.float32
    r = jax.nn.sigmoid(block_diag_linear(x, wa, ba).astype(f32))
    i = jax.nn.sigmoid(block_diag_linear(x, wx, bx).astype(f32))
    log_a = -LRU_C * r * jax.nn.softplus(-lam.astype(f32))
    a = jnp.exp(log_a)
    gated = jnp.sqrt(-jnp.expm1(2.0 * log_a)) * i * x.astype(f32)

    def combine(lhs, rhs):
        a1, b1 = lhs
        a2, b2 = rhs
        return a1 * a2, a2 * b1 + b2

    _, h = lax.associative_scan(combine, (a, gated), axis=1)
    return h.astype(x.dtype)


def nsa_attention(q, k_c, v_c, k_s, v_s, k_w, v_w, gate_logits, gate_b, rel_table,
                  pe_k, pe_v, w1_k, w1_v, w2_k, w2_v):
    f32 = jnp.float32
    bsz, seq, _ = q.shape
    dk = NSA_HEAD_DIM
    pos = jnp.arange(seq)
    tab = rel_table.reshape(REL_BUCKETS, NSA_KV_HEADS, NSA_GROUP)

    def heads_kv(t):
        return t.reshape(bsz, seq, NSA_KV_HEADS, dk).transpose(0, 2, 1, 3)

    qh = q.reshape(bsz, seq, NSA_KV_HEADS, NSA_GROUP, dk).transpose(0, 2, 3, 1, 4) * (dk ** -0.5)
    k_c, v_c, k_s, v_s, k_w, v_w = (heads_kv(t) for t in (k_c, v_c, k_s, v_s, k_w, v_w))

    def compress(t, pe, w1, w2):
        t16 = t.reshape(bsz, NSA_KV_HEADS, seq // CMP_STRIDE, CMP_STRIDE, dk)
        blocks = jnp.concatenate([t16[:, :, :-1], t16[:, :, 1:]], axis=3) + pe
        flat = blocks.reshape(blocks.shape[:3] + (CMP_BLOCK * dk,))
        return jax.nn.silu(flat @ w1) @ w2

    kc = compress(k_c, pe_k, w1_k, w2_k)
    vc = compress(v_c, pe_v, w1_v, w2_v)
    n_cmp = kc.shape[2]
    c_start = jnp.arange(n_cmp) * CMP_STRIDE
    dist_c = pos[:, None] - (c_start + CMP_BLOCK - 1)[None, :]
    bias_c = jnp.moveaxis(tab[rel_bucket(dist_c)], (-2, -1), (0, 1))
    s_c = jnp.einsum('bhgsd,bhnd->bhgsn', qh, kc).astype(f32) + bias_c
    p_c = masked_softmax(s_c, dist_c >= 0)
    o_c = jnp.einsum('bhgsn,bhnd->bhgsd', p_c.astype(vc.dtype), vc)

    n_slc = seq // SLC_BLOCK
    s_start = jnp.arange(n_slc) * SLC_BLOCK
    overlap = jnp.clip(
        jnp.minimum(c_start[:, None] + CMP_BLOCK, s_start[None, :] + SLC_BLOCK)
        - jnp.maximum(c_start[:, None], s_start[None, :]), 0, None).astype(f32) / CMP_BLOCK
    imp = jnp.einsum('bhgsn,nj->bhsj', p_c, overlap)
    q_blk = pos // SLC_BLOCK
    j = jnp.arange(n_slc)
    forced = (j[None, :] == 0) | (j[None, :] == q_blk[:, None]) | (j[None, :] == q_blk[:, None] - 1)
    future = j[None, :] > q_blk[:, None]
    imp = jnp.where(future, -jnp.inf, jnp.where(forced, jnp.inf, imp))
    top_k = min(SLC_TOPK, n_slc)
    _, sel = lax.top_k(imp, top_k)

    k_sb = k_s.reshape(bsz, NSA_KV_HEADS, n_slc, SLC_BLOCK, dk)
    v_sb = v_s.reshape(bsz, NSA_KV_HEADS, n_slc, SLC_BLOCK, dk)
    k_wp = jnp.pad(k_w, ((0, 0), (0, 0), (WINDOW, 0), (0, 0)))
    v_wp = jnp.pad(v_w, ((0, 0), (0, 0), (WINDOW, 0), (0, 0)))
    n_qb = seq // Q_BLOCK
    b_ix = jnp.arange(bsz)[:, None, None, None]
    h_ix = jnp.arange(NSA_KV_HEADS)[None, :, None, None]
    tab_kv = jnp.transpose(tab, (1, 0, 2))
    offs = jnp.arange(SLC_BLOCK)
    win_offs = jnp.arange(WINDOW + Q_BLOCK)

    def block_step(args):
        qb, q_b, sel_b = args
        tq = qb * Q_BLOCK + jnp.arange(Q_BLOCK)
        ks = k_sb[b_ix, h_ix, sel_b]
        vs = v_sb[b_ix, h_ix, sel_b]
        kpos = sel_b[..., None] * SLC_BLOCK + offs
        dist = tq[:, None, None] - kpos
        bias = tab_kv[h_ix[..., None], rel_bucket(dist)]
        bias = jnp.moveaxis(bias, -1, 2).reshape(bsz, NSA_KV_HEADS, NSA_GROUP, Q_BLOCK, top_k * SLC_BLOCK)
        s = jnp.einsum('bhgqd,bhqnkd->bhgqnk', q_b, ks).astype(f32).reshape(
            bsz, NSA_KV_HEADS, NSA_GROUP, Q_BLOCK, top_k * SLC_BLOCK) + bias
        mask = (dist >= 0).reshape(bsz, NSA_KV_HEADS, 1, Q_BLOCK, top_k * SLC_BLOCK)
        p = masked_softmax(s, mask).reshape(bsz, NSA_KV_HEADS, NSA_GROUP, Q_BLOCK, top_k, SLC_BLOCK)
        o_s = jnp.einsum('bhgqnk,bhqnkd->bhgqd', p.astype(vs.dtype), vs)

        kw = lax.dynamic_slice_in_dim(k_wp, qb * Q_BLOCK, WINDOW + Q_BLOCK, axis=2)
        vw = lax.dynamic_slice_in_dim(v_wp, qb * Q_BLOCK, WINDOW + Q_BLOCK, axis=2)
        kpos_w = qb * Q_BLOCK - WINDOW + win_offs
        dist_w = tq[:, None] - kpos_w[None, :]
        mask_w = (dist_w >= 0) & (dist_w < WINDOW) & (kpos_w[None, :] >= 0)
        bias_w = jnp.moveaxis(tab[rel_bucket(dist_w)], (-2, -1), (0, 1))
        s_w = jnp.einsum('bhgqd,bhkd->bhgqk', q_b, kw).astype(f32) + bias_w
        p_w = masked_softmax(s_w, mask_w)
        o_w = jnp.einsum('bhgqk,bhkd->bhgqd', p_w.astype(vw.dtype), vw)
        return o_s, o_w

    q_blocks = jnp.moveaxis(qh.reshape(bsz, NSA_KV_HEADS, NSA_GROUP, n_qb, Q_BLOCK, dk), 3, 0)
    sel_blocks = jnp.moveaxis(sel.reshape(bsz, NSA_KV_HEADS, n_qb, Q_BLOCK, top_k), 2, 0)
    o_s, o_w = lax.map(block_step, (jnp.arange(n_qb), q_blocks, sel_blocks))
    o_s = jnp.moveaxis(o_s, 0, 3).reshape(bsz, NSA_KV_HEADS, NSA_GROUP, seq, dk)
    o_w = jnp.moveaxis(o_w, 0, 3).reshape(bsz, NSA_KV_HEADS, NSA_GROUP, seq, dk)

    g = jax.nn.sigmoid(gate_logits.astype(f32) + gate_b)
    g = g.reshape(bsz, seq, 3, NSA_KV_HEADS, NSA_GROUP).transpose(2, 0, 3, 4, 1)[..., None]
    o = g[0] * o_c + g[1] * o_s + g[2] * o_w
    return o.transpose(0, 3, 1, 2, 4).reshape(bsz, seq, D_BR).astype(q.dtype)


def memory_attention(q, mem, w_kv):
    bsz, seq, _ = q.shape
    k, v = jnp.split(mem @ w_kv, 2, axis=-1)
    k = k.reshape(bsz, -1, MEM_HEADS, MEM_HEAD_DIM)
    v = v.reshape(bsz, -1, MEM_HEADS, MEM_HEAD_DIM)
    qh = q.reshape(bsz, seq, MEM_HEADS, MEM_HEAD_DIM) * (MEM_HEAD_DIM ** -0.5)
    p = jax.nn.softmax(jnp.einsum('bshd,bmhd->bhsm', qh, k).astype(jnp.float32), axis=-1)
    return jnp.einsum('bhsm,bmhd->bshd', p.astype(v.dtype), v).reshape(bsz, seq, D_BR)


def setup_inputs(seed: int = 0) -> dict:
    key = jax.random.key(seed)
    ks = jax.random.split(key, 32)
    f32 = jnp.float32
    dk = NSA_HEAD_DIM

    def nrm(k, shape, scale):
        return jax.random.normal(k, shape, f32) * scale

    a_pow = jax.random.uniform(ks[14], (DEPTH, D_BR), f32, 0.9, 0.999)
    a_base = a_pow ** (1.0 / LRU_C)
    lru_lambda = jnp.log(a_base) - jnp.log1p(-a_base)
    return {
        "x": nrm(ks[0], (BATCH, SEQ, D_MODEL), 1.0),
        "mem": nrm(ks[1], (BATCH, MEM_LEN, D_MODEL), 1.0),
        "rel_bias": nrm(ks[2], (REL_BUCKETS, NSA_HEADS), 0.5),
        "w_in": nrm(ks[3], (DEPTH, D_MODEL, IN_WIDTH), D_MODEL ** -0.5),
        "sgu_ln_g": 1.0 + nrm(ks[4], (DEPTH, D_BR), 0.01),
        "sgu_ln_b": nrm(ks[5], (DEPTH, D_BR), 0.01),
        "sgu_w": nrm(ks[6], (DEPTH, GMLP_GROUPS, GMLP_CHUNK, GMLP_CHUNK), GMLP_CHUNK ** -0.5),
        "sgu_b": 1.0 + nrm(ks[7], (DEPTH, GMLP_GROUPS, GMLP_CHUNK), 0.01),
        "conv_w": nrm(ks[8], (DEPTH, CONV_WIDTH, D_BR), CONV_WIDTH ** -0.5),
        "conv_b": nrm(ks[9], (DEPTH, D_BR), 0.01),
        "lru_wa": nrm(ks[10], (DEPTH, LRU_BLOCKS, LRU_BLOCK_DIM, LRU_BLOCK_DIM), LRU_BLOCK_DIM ** -0.5),
        "lru_ba": nrm(ks[11], (DEPTH, D_BR), 0.01),
        "lru_wx": nrm(ks[12], (DEPTH, LRU_BLOCKS, LRU_BLOCK_DIM, LRU_BLOCK_DIM), LRU_BLOCK_DIM ** -0.5),
        "lru_bx": nrm(ks[13], (DEPTH, D_BR), 0.01),
        "lru_lambda": lru_lambda,
        "cmp_pe_k": nrm(ks[15], (DEPTH, CMP_BLOCK, dk), 0.1),
        "cmp_pe_v": nrm(ks[16], (DEPTH, CMP_BLOCK, dk), 0.1),
        "cmp_w1_k": nrm(ks[17], (DEPTH, CMP_BLOCK * dk, CMP_HIDDEN), (CMP_BLOCK * dk) ** -0.5),
        "cmp_w1_v": nrm(ks[18], (DEPTH, CMP_BLOCK * dk, CMP_HIDDEN), (CMP_BLOCK * dk) ** -0.5),
        "cmp_w2_k": nrm(ks[19], (DEPTH, CMP_HIDDEN, dk), CMP_HIDDEN ** -0.5),
        "cmp_w2_v": nrm(ks[20], (DEPTH, CMP_HIDDEN, dk), CMP_HIDDEN ** -0.5),
        "nsa_gate_b": nrm(ks[21], (DEPTH, 3 * NSA_HEADS), 0.01),
        "w_mem_kv": nrm(ks[22], (DEPTH, D_MODEL, 2 * D_BR), D_MODEL ** -0.5),
        "w_branch": nrm(ks[23], (DEPTH, N_BRANCHES, D_BR, D_MODEL), DN_BETA * D_BR ** -0.5),
        "w_out": nrm(ks[24], (DEPTH, D_MODEL, D_MODEL), DN_BETA * D_MODEL ** -0.5),
        "ln_g": 1.0 + nrm(ks[25], (DEPTH, D_MODEL), 0.01),
        "ln_b": nrm(ks[26], (DEPTH, D_MODEL), 0.01),
    }


def reference(x, mem, rel_bias, w_in, sgu_ln_g, sgu_ln_b, sgu_w, sgu_b, conv_w, conv_b,
              lru_wa, lru_ba, lru_wx, lru_bx, lru_lambda, cmp_pe_k, cmp_pe_v, cmp_w1_k, cmp_w1_v,
              cmp_w2_k, cmp_w2_v, nsa_gate_b, w_mem_kv, w_branch, w_out, ln_g, ln_b):
    bsz, seq, _ = x.shape
    split_points = in_split_points()
    for l in range(DEPTH):
        h = x @ w_in[l]
        (u_a, v_a, g_a, x_b, g_b, q_c, kc_c, vc_c, ks_c, vs_c, kw_c, vw_c, gl_c, g_c,
         q_m, g_m, g_merge) = jnp.split(h, split_points, axis=-1)

        o_a = gmlp_spatial_gating(u_a, v_a, sgu_ln_g[l], sgu_ln_b[l], sgu_w[l], sgu_b[l]) * jax.nn.silu(g_a)
        x_conv = causal_depthwise_conv(x_b, conv_w[l], conv_b[l])
        o_b = rg_lru(x_conv, lru_wa[l], lru_ba[l], lru_wx[l], lru_bx[l], lru_lambda[l]) * jax.nn.silu(g_b)
        o_c = nsa_attention(q_c, kc_c, vc_c, ks_c, vs_c, kw_c, vw_c, gl_c, nsa_gate_b[l], rel_bias,
                            cmp_pe_k[l], cmp_pe_v[l], cmp_w1_k[l], cmp_w1_v[l],
                            cmp_w2_k[l], cmp_w2_v[l]) * jax.nn.silu(g_c)
        o_m = memory_attention(q_m, mem, w_mem_kv[l]) * jax.nn.silu(g_m)

        gates = jax.nn.sigmoid(g_merge.reshape(bsz, seq, N_BRANCHES, D_MODEL))
        merged = gates[:, :, 0] * (o_a @ w_branch[l, 0])
        merged = merged + gates[:, :, 1] * (o_b @ w_branch[l, 1])
        merged = merged + gates[:, :, 2] * (o_c @ w_branch[l, 2])
        merged = merged + gates[:, :, 3] * (o_m @ w_branch[l, 3])
        y = merged @ w_out[l]
        x = layer_norm(DN_ALPHA * x + y, ln_g[l], ln_b[l])
    return x
```

```python
import os
import contextlib
import numpy as np
import concourse.bass as bass
import concourse.mybir as mybir

F32 = mybir.dt.float32
BF16 = mybir.dt.bfloat16
AF = mybir.ActivationFunctionType
ALU = mybir.AluOpType
AX = mybir.AxisListType

ENGS = ("pe", "act", "dve", "pool", "sp")
SEM_LIMIT = 20000
N_DMA_SEMS = 16


class Prog:
    def __init__(self, nc):
        self.nc = nc
        self.stack = contextlib.ExitStack()
        self.streams = {e: [] for e in ENGS}
        self.cur_sem = {}
        self._emitted_wait = {e: {} for e in ENGS}
        self.last_w = {}
        self.readers = {}
        self.dma_sems = None
        self.dma_rr = 0
        self.dma_cnt = []
        self.nsem = 0
        self.all_dma_tokens = []
        self.extra_tokens = []
        self._cap = None

    def sem(self, name):
        self.nsem += 1
        return self.stack.enter_context(self.nc.semaphore(f"{name}_{self.nsem}"))

    def sbuf(self, name, shape, dt):
        return self.stack.enter_context(self.nc.sbuf_tensor(name, list(shape), dt))

    def psum(self, name, shape, dt=F32):
        return self.stack.enter_context(self.nc.psum_tensor(name, list(shape), dt))

    def _eng_token(self, e):
        if e not in self.cur_sem or self.cur_sem[e][1] >= SEM_LIMIT:
            self.cur_sem[e] = [self.sem("e" + e), 0]
        cs = self.cur_sem[e]
        cs[1] += 1
        return (cs[0], cs[1])

    def _deps(self, e, reads, writes, same_engine_ok):
        deps = []
        for r in reads:
            t = self.last_w.get(r)
            if t is not None:
                deps.append(t)
        for w in writes:
            t = self.last_w.get(w)
            if t is not None:
                deps.append(t)
            deps.extend(self.readers.get(w, ()))
        need = {}
        for (s, v, src) in deps:
            if same_engine_ok and src == e:
                continue
            k = id(s)
            if k not in need or need[k][1] < v:
                need[k] = (s, v)
        out = []
        ew = self._emitted_wait[e]
        for k, (s, v) in need.items():
            if ew.get(k, 0) >= v:
                continue
            out.append((s, v))
            ew[k] = v
        return out

    def _record(self, tok, reads, writes):
        for r in reads:
            self.readers.setdefault(r, []).append(tok)
        for w in writes:
            self.last_w[w] = tok
            self.readers[w] = []

    def op(self, e, fn, reads=(), writes=(), same_ok=None):
        if self._cap is not None:
            self._cap.append(("op", (e, fn, tuple(reads), tuple(writes), same_ok), {}))
            return
        if same_ok is None:
            same_ok = (e == "pe")
        waits = self._deps(e, reads, writes, same_ok)
        s, v = self._eng_token(e)
        self.streams[e].append((waits, fn, (s, 1)))
        self._record((s, v, e), reads, writes)

    def dma(self, q, out, in_, reads=(), writes=(), **kw):
        if self._cap is not None:
            self._cap.append(("dma", (q, out, in_, tuple(reads), tuple(writes)), dict(kw)))
            return
        if self.dma_sems is None:
            self.dma_sems = [self.sem("dma") for _ in range(N_DMA_SEMS)]
            self.dma_cnt = [0] * N_DMA_SEMS
        i = self.dma_rr
        self.dma_rr = (self.dma_rr + 1) % N_DMA_SEMS
        s = self.dma_sems[i]
        waits = self._deps(q, reads, writes, False)
        c = self.dma_cnt[i]
        if c > 0 and self._emitted_wait[q].get(id(s), 0) < 16 * c:
            waits.append((s, 16 * c))
            self._emitted_wait[q][id(s)] = 16 * c
        self.dma_cnt[i] = c + 1
        tok = (s, 16 * (c + 1), "dma")
        def fn(eng, out=out, in_=in_, kw=kw):
            o = out(eng) if callable(out) else out
            i = in_(eng) if callable(in_) else in_
            return eng.dma_start(out=o, in_=i, **kw)
        self.streams[q].append((waits, fn, (s, 16)))
        self._record(tok, reads, writes)
        self.all_dma_tokens.append(tok)


    def capture(self):
        self._cap = []

    def end_capture(self):
        c = self._cap
        self._cap = None
        return c

    def replay(self, lists, width):
        active = []
        i = 0
        while i < len(lists) or active:
            while len(active) < width and i < len(lists):
                if lists[i]:
                    active.append([lists[i], 0])
                i += 1
            for a in list(active):
                kind, args, kw = a[0][a[1]]
                a[1] += 1
                if kind == "op":
                    self.op(*args)
                else:
                    self.dma(*args, **kw)
                if a[1] >= len(a[0]):
                    active.remove(a)

    def par(self, eng):
        k = id(eng)
        if not hasattr(self, "_par"):
            self._par = {}
        if k not in self._par:
            self._par[k] = eng.partition_id() % 2
        return self._par[k]

    def collective(self, kind, replica_groups, in_ap, out_ap):
        s = self.sem("cc")
        fn = lambda eng: eng.collective_compute(kind, mybir.AluOpType.bypass, replica_groups=replica_groups, ins=[in_ap], outs=[out_ap])
        self.streams["pool"].append(([], fn, (s, 1)))
        self.extra_tokens.append((s, 1))

    def barrier(self):
        toks = []
        for e, cs in self.cur_sem.items():
            toks.append((cs[0], cs[1]))
        if self.dma_sems is not None:
            for s, c in zip(self.dma_sems, self.dma_cnt):
                if c > 0:
                    toks.append((s, 16 * c))
        toks.extend(self.extra_tokens)
        for e in ENGS:
            waits = []
            ew = self._emitted_wait[e]
            for (s, v) in toks:
                if ew.get(id(s), 0) >= v:
                    continue
                waits.append((s, v))
                ew[id(s)] = v
            if waits:
                self.streams[e].append((waits, None, None))
        self.last_w = {}
        self.readers = {}

    def finish(self):
        waits = []
        if self.dma_sems is not None:
            for s, c in zip(self.dma_sems, self.dma_cnt):
                if c > 0:
                    waits.append((s, 16 * c))
        self.streams["sp"].append((waits, None, None))

    def emit(self):
        nc = self.nc
        self.finish()
        with nc.Block() as block:
            def run(eng, stream):
                for waits, fn, inc in stream:
                    for (s, v) in waits:
                        eng.wait_ge(s, v)
                    if fn is not None:
                        ins = fn(eng)
                        ins.then_inc(inc[0], inc[1])

            @block.tensor
            def _(eng):
                run(eng, self.streams["pe"])

            @block.scalar
            def _(eng):
                run(eng, self.streams["act"])

            @block.vector
            def _(eng):
                run(eng, self.streams["dve"])

            @block.gpsimd
            def _(eng):
                run(eng, self.streams["pool"])

            @block.sync
            def _(eng):
                run(eng, self.streams["sp"])
        self.stack.close()


D = 2048
ALPHA = 4 ** 0.25
LN_EPS = 1e-5


class Arena:
    def __init__(self, p, name, words):
        self.t = p.sbuf(name, [128, words], F32)
        self.words = words
        self.off = 0

    def reset(self):
        self.off = 0

    def alloc(self, shape, dt):
        n = 1
        for s in shape:
            n *= s
        nbytes = n * (4 if dt == F32 else 2)
        nw = (nbytes + 3) // 4
        nw = (nw + 7) // 8 * 8
        assert self.off + nw <= self.words, (self.off, nw, self.words)
        ap = self.t[:, self.off:self.off + nw]
        self.off += nw
        if dt != F32:
            ap = ap.bitcast(dt)
        ap = ap[:, 0:n]
        if len(shape) > 1:
            names = " ".join(f"a{i}" for i in range(len(shape)))
            kw = {f"a{i}": shape[i] for i in range(len(shape))}
            ap = ap.rearrange(f"p ({names}) -> p {names}", **kw)
        return ap


def build_s2(nc, p, ar, ps, T2=2048, TP=1024, sfx="", x_rows=None, o_load=None, y_store=None):
    nI = lambda n, s, d=F32: nc.dram_tensor(n + sfx, s, d, kind="ExternalInput").ap()
    if x_rows is None:
        x_tok = nI("x_tok", [T2, D])
        o_tok = nI("o_tok", [T2, 2 * D], BF16)
        x_rows = lambda r0: x_tok[r0:r0 + 128, :]
        o_load = lambda lbuf, lk, r0: p.dma("pool", lbuf, o_tok[r0:r0 + 128, :], writes=[lk])
    wg = nI("wg", [64, 128, 16 * 128])
    wb = nI("wb", [64, 128, 8 * 128])
    wo = nI("wo", [4, 128, 16 * 512])
    lng = nI("lng", [128, D])
    lnb = nI("lnb", [128, D])
    ident_d = nI("ident", [128, 128])
    if y_store is None:
        y_out = nc.dram_tensor("y_out" + sfx, [T2, D], F32, kind="ExternalOutput").ap()
        y_store = lambda r_, rk, r0: p.dma("sp", y_out[r0:r0 + 128, :], r_, reads=[rk])

    ar.reset()
    ident = ar.alloc([128], BF16)
    p.dma("pool", ident, ident_d, writes=["ident"])
    lg = ar.alloc([D], F32); lb = ar.alloc([D], F32)
    p.dma("sp", lg, lng, writes=["lg"]); p.dma("sp", lb, lnb, writes=["lb"])
    epsT = ar.alloc([1], F32)
    p.op("pool", lambda e: e.memset(epsT, LN_EPS), writes=["eps"])
    NT = TP // 128
    xT = ar.alloc([16, TP], BF16)
    mT = ar.alloc([16, TP], BF16)
    big = ar.alloc([32 * TP], BF16)
    oT = big.rearrange("p (f t) -> p f t", f=32)
    woS = big.rearrange("p (k n) -> p k n", k=16)
    assert 32 * TP == 16 * 2048
    mark = ar.off
    ldb = [ar.alloc([2 * D], BF16) for _ in range(2)]
    ar.off = mark
    wgS = [ar.alloc([16, 128], BF16) for _ in range(2)]
    wbS = [ar.alloc([8, 128], BF16) for _ in range(2)]
    sg = [ar.alloc([512], F32) for _ in range(2)]
    tmp = [ar.alloc([512], F32) for _ in range(2)]
    macc = [ar.alloc([512], F32) for _ in range(TP // 512)]
    ar.off = mark
    x32 = [ar.alloc([D], F32) for _ in range(2)]
    rb = [ar.alloc([D], F32) for _ in range(2)]
    st = [ar.alloc([4, 6], F32) for _ in range(2)]
    mv = [ar.alloc([2], F32) for _ in range(2)]
    rstd = [ar.alloc([1], F32) for _ in range(2)]
    tpb = [ps[4].bitcast(BF16), ps[5].bitcast(BF16)]

    cnt = {"ev": 0}

    def evac_copy(out, in_, reads, writes):
        cnt["ev"] += 1
        if cnt["ev"] % 2:
            p.op("act", lambda e: e.copy(out=out, in_=in_), reads=reads, writes=writes)
        else:
            p.op("dve", lambda e: e.tensor_copy(out=out, in_=in_), reads=reads, writes=writes)

    for ps_i in range(T2 // TP):
        tok0 = ps_i * TP
        for t in range(NT):
            r0 = tok0 + t * 128
            for which, nf, dstT, key in (("x", 16, xT, "xT"), ("o", 32, oT, "big")):
                lbuf = ldb[(2 * t + (which == "o")) % 2]
                lk = "ldb%d" % ((2 * t + (which == "o")) % 2)
                if which == "x":
                    p.dma("pool", lbuf[:, 0:nf * 128], x_rows(r0), writes=[lk])
                else:
                    o_load(lbuf, lk, r0)
                for g in range(nf // 4):
                    tp = tpb[g % 2]; tk = "PS%d" % (4 + g % 2)

                    def tr(e, tp=tp, lbuf=lbuf, g=g):
                        for q in range(4):
                            f = g * 4 + q
                            ins = e.transpose(out=tp[:, q * 128:(q + 1) * 128], in_=lbuf[:, f * 128:(f + 1) * 128], identity=ident)
                        return ins
                    p.op("pe", tr, reads=[lk, "ident"], writes=[tk])
                    evac_copy(dstT[:, g * 4:(g + 1) * 4, t * 128:(t + 1) * 128],
                              tp[:, 0:512].rearrange("p (q n) -> p q n", q=4), [tk], [key])
        p.barrier()
        import os
        S2STOP = int(os.environ.get('S2STOP', '3'))
        it = 0
        for c in range(16 if S2STOP >= 2 else 0):
            for i in range(4):
                sl = c * 4 + i
                wgs = wgS[sl % 2]; wbs = wbS[sl % 2]; kg = "wg%d" % (sl % 2); kb = "wb%d" % (sl % 2)
                p.dma("pool", wgs, wg[c * 4 + i].rearrange("p (k n) -> p k n", k=16), writes=[kg])
                p.dma("pool", wbs, wb[c * 4 + i].rearrange("p (k n) -> p k n", k=8), writes=[kb])
                for tt in range(TP // 512):
                    G = ps[it % 2]; B = ps[2 + it % 2]; gk = "PS%d" % (it % 2); bk = "PS%d" % (2 + it % 2)
                    tsl = slice(tt * 512, (tt + 1) * 512)

                    def mmG(e, G=G, wgs=wgs, tsl=tsl):
                        for kc in range(16):
                            ins = e.matmul(G, lhsT=wgs[:, kc, :], rhs=xT[:, kc, tsl], start=(kc == 0), stop=(kc == 15))
                        return ins

                    def mmB(e, B=B, wbs=wbs, tsl=tsl, i=i):
                        for kc in range(8):
                            ins = e.matmul(B, lhsT=wbs[:, kc, :], rhs=oT[:, i * 8 + kc, tsl], start=(kc == 0), stop=(kc == 7))
                        return ins
                    p.op("pe", mmG, reads=[kg, "xT"], writes=[gk])
                    p.op("pe", mmB, reads=[kb, "big"], writes=[bk])
                    sgb = sg[it % 2]; sk = "sg%d" % (it % 2)
                    p.op("act", lambda e, sgb=sgb, G=G: e.activation(out=sgb, in_=G, func=AF.Sigmoid), reads=[gk], writes=[sk])
                    mk = "macc%d" % tt
                    if i == 0:
                        p.op("dve", lambda e, sgb=sgb, B=B, m=macc[tt]: e.tensor_tensor(out=m, in0=sgb, in1=B, op=ALU.mult),
                             reads=[sk, bk], writes=[mk])
                    else:
                        tb = tmp[it % 2]; tk2 = "tmp%d" % (it % 2)
                        p.op("dve", lambda e, sgb=sgb, B=B, tb=tb: e.tensor_tensor(out=tb, in0=sgb, in1=B, op=ALU.mult),
                             reads=[sk, bk], writes=[tk2])
                        if i < 3:
                            p.op("dve", lambda e, tb=tb, m=macc[tt]: e.tensor_tensor(out=m, in0=m, in1=tb, op=ALU.add),
                                 reads=[tk2, mk], writes=[mk])
                        else:
                            p.op("dve", lambda e, tb=tb, m=macc[tt], c=c, tsl=tsl: e.tensor_tensor(out=mT[:, c, tsl], in0=m, in1=tb, op=ALU.add),
                                 reads=[tk2, mk], writes=["mT"])
                    it += 1
        p.barrier()
        for nt in range(4):
            p.dma("pool", woS[:, :, nt * 512:(nt + 1) * 512], wo[nt].rearrange("p (k n) -> p k n", k=16), writes=["big"])
        y_lists = []
        for t in range(NT if S2STOP >= 3 else 0):
            r0 = tok0 + t * 128
            p.capture()
            xb_ = x32[t % 2]; xk = "x32%d" % (t % 2)
            r_ = rb[t % 2]; rk = "rb%d" % (t % 2)
            p.dma("sp", xb_, x_rows(r0), writes=[xk])
            for nt in range(4):
                Y = ps[4 + (t % 2) * 2 + nt % 2]; yk = "PS%d" % (4 + (t % 2) * 2 + nt % 2)

                def mmY(e, Y=Y, t=t, nt=nt):
                    for kc in range(16):
                        ins = e.matmul(Y, lhsT=mT[:, kc, t * 128:(t + 1) * 128], rhs=woS[:, kc, nt * 512:(nt + 1) * 512],
                                       start=(kc == 0), stop=(kc == 15))
                    return ins
                p.op("pe", mmY, reads=["mT", "big"], writes=[yk])
                p.op("dve", lambda e, Y=Y, xb_=xb_, r_=r_, nt=nt: e.scalar_tensor_tensor(
                    out=r_[:, nt * 512:(nt + 1) * 512], in0=xb_[:, nt * 512:(nt + 1) * 512], scalar=ALPHA, in1=Y,
                    op0=ALU.mult, op1=ALU.add), reads=[yk, xk], writes=[rk])
            s_ = st[t % 2]; sk = "st%d" % (t % 2); m_ = mv[t % 2]; mk2 = "mv%d" % (t % 2); rs = rstd[t % 2]; rsk = "rstd%d" % (t % 2)

            def stats(e, s_=s_, r_=r_):
                for q in range(4):
                    ins = e.bn_stats(out=s_[:, q, :], in_=r_[:, q * 512:(q + 1) * 512])
                return ins
            p.op("dve", stats, reads=[rk], writes=[sk])
            p.op("dve", lambda e, s_=s_, m_=m_: e.bn_aggr(out=m_, in_=s_.rearrange("p a b -> p (a b)")), reads=[sk], writes=[mk2])
            p.op("act", lambda e, m_=m_, rs=rs: e.activation(out=rs, in_=m_[:, 1:2], func=AF.Sqrt, bias=epsT, scale=1.0),
                 reads=[mk2, "eps"], writes=[rsk])
            p.op("dve", lambda e, rs=rs: e.reciprocal(out=rs, in_=rs), reads=[rsk], writes=[rsk])
            p.op("dve", lambda e, r_=r_, m_=m_, rs=rs: e.tensor_scalar(out=r_, in0=r_, scalar1=m_[:, 0:1], scalar2=rs,
                                                                    op0=ALU.subtract, op1=ALU.mult), reads=[rk, mk2, rsk], writes=[rk])
            p.op("pool", lambda e, r_=r_: e.tensor_tensor(out=r_, in0=r_, in1=lg, op=ALU.mult), reads=[rk, "lg"], writes=[rk])
            p.op("pool", lambda e, r_=r_: e.tensor_tensor(out=r_, in0=r_, in1=lb, op=ALU.add), reads=[rk, "lb"], writes=[rk])
            y_store(r_, rk, r0)
            y_lists.append(p.end_capture())
        p.replay(y_lists, 2)
        p.barrier()


S = 4096
D = 2048
NFM = 20
NTM = 7
GC = 1.5957691216057308


def build_s1(nc, p, ar, ps, sfx="", phases=("A", "G", "L", "M", "N"), x_src=None, o_dst=None, x_tile_row=None):
    nI = lambda n, s, d=F32: nc.dram_tensor(n + sfx, s, d, kind="ExternalInput").ap()
    x_tok = nI("x_tok1", [S, D]) if x_src is None else x_src
    wfm = nI("wfm", [NFM, 128, 16 * 128])
    wtm = nI("wtm", [NTM, 128, 16 * 512])
    ident_d = nI("ident1", [128, 128])
    o_out = nc.dram_tensor("o_out" + sfx, [S, 2048], BF16, kind="ExternalOutput").ap() if o_dst is None else o_dst
    fmb = nc.dram_tensor("fmb" + sfx, [12, 128, S], BF16).ap()
    fmf = nc.dram_tensor("fmf" + sfx, [8, 128, S], F32).ap()
    tm = nc.dram_tensor("tm" + sfx, [S, NTM * 512], F32).ap()
    cnt = {"ev": 0}

    def evac(out, in_, reads, writes, scale=None):
        cnt["ev"] += 1
        if cnt["ev"] % 2:
            if scale is None:
                p.op("act", lambda e: e.copy(out=out, in_=in_), reads=reads, writes=writes)
            else:
                p.op("act", lambda e: e.mul(out=out, in_=in_, mul=scale) if False else e.activation(out=out, in_=in_, func=AF.Copy, scale=scale), reads=reads, writes=writes)
        else:
            if scale is None:
                p.op("dve", lambda e: e.tensor_copy(out=out, in_=in_), reads=reads, writes=writes)
            else:
                p.op("dve", lambda e: e.tensor_scalar(out=out, in0=in_, scalar1=scale, scalar2=None, op0=ALU.mult), reads=reads, writes=writes)

    if "A" in phases:
        ar.reset()
        ident = ar.alloc([128], BF16)
        p.dma("pool", ident, ident_d, writes=["ident"])
        xT = ar.alloc([16, S], BF16)
        ldb = [ar.alloc([D], BF16) for _ in range(2)]
        tpb = [ps[4].bitcast(BF16), ps[5].bitcast(BF16)]
        for t in range(S // 128):
            lbuf = ldb[t % 2]; lk = "ldb%d" % (t % 2)
            xr0 = t * 128 if x_tile_row is None else x_tile_row(t)
            p.dma("pool", lbuf, x_tok[xr0:xr0 + 128, :], writes=[lk])
            for g in range(4):
                tp = tpb[g % 2]; tk = "PS%d" % (4 + g % 2)

                def tr(e, tp=tp, lbuf=lbuf, g=g):
                    for q in range(4):
                        f = g * 4 + q
                        ins = e.transpose(out=tp[:, q * 128:(q + 1) * 128], in_=lbuf[:, f * 128:(f + 1) * 128], identity=ident)
                    return ins
                p.op("pe", tr, reads=[lk, "ident"], writes=[tk])
                evac(xT[:, g * 4:(g + 1) * 4, t * 128:(t + 1) * 128], tp[:, 0:512].rearrange("p (q n) -> p q n", q=4), [tk], ["xT"])
        wS = [ar.alloc([16, 128], BF16) for _ in range(2)]
        stb = [ar.alloc([512], BF16) for _ in range(4)]
        stf = [ar.alloc([512], F32) for _ in range(4)]
        it = 0
        for ft in range(NFM):
            w_ = wS[ft % 2]; wk = "wS%d" % (ft % 2)
            p.dma("pool", w_, wfm[ft].rearrange("p (k n) -> p k n", k=16), writes=[wk])
            for Tq in range(S // 512):
                P_ = ps[it % 4]; pk = "PS%d" % (it % 4)

                def mm(e, P_=P_, w_=w_, Tq=Tq):
                    for kc in range(16):
                        ins = e.matmul(P_, lhsT=w_[:, kc, :], rhs=xT[:, kc, Tq * 512:(Tq + 1) * 512], start=(kc == 0), stop=(kc == 15))
                    return ins
                p.op("pe", mm, reads=[wk, "xT"], writes=[pk])
                if ft < 12:
                    sb = stb[it % 4]; sk = "stb%d" % (it % 4)
                    sc = 0.125 if ft < 4 else (1.0 / 16 if 8 <= ft < 12 else None)
                    evac(sb, P_, [pk], [sk], scale=sc)
                    p.dma("sp", fmb[ft, :, Tq * 512:(Tq + 1) * 512], sb, reads=[sk])
                else:
                    sb = stf[it % 4]; sk = "stf%d" % (it % 4)
                    evac(sb, P_, [pk], [sk])
                    p.dma("sp", fmf[ft - 12, :, Tq * 512:(Tq + 1) * 512], sb, reads=[sk])
                it += 1
        wT_ = [ar.alloc([16, 512], BF16) for _ in range(2)]
        for tg in range(NTM):
            w_ = wT_[tg % 2]; wk = "wT%d" % (tg % 2)
            p.dma("pool", w_, wtm[tg].rearrange("p (k n) -> p k n", k=16), writes=[wk])
            ncol = 512 if tg < 6 else 280
            for t in range(S // 128):
                P_ = ps[it % 4]; pk = "PS%d" % (it % 4)

                def mm(e, P_=P_, w_=w_, t=t, ncol=ncol):
                    for kc in range(16):
                        ins = e.matmul(P_[:, 0:ncol], lhsT=xT[:, kc, t * 128:(t + 1) * 128], rhs=w_[:, kc, 0:ncol], start=(kc == 0), stop=(kc == 15))
                    return ins
                p.op("pe", mm, reads=[wk, "xT"], writes=[pk])
                sb = stf[it % 4]; sk = "stf%d" % (it % 4)
                evac(sb[:, 0:ncol], P_[:, 0:ncol], [pk], [sk])
                p.dma("sp", tm[t * 128:(t + 1) * 128, tg * 512:tg * 512 + ncol], sb[:, 0:ncol], reads=[sk])
                it += 1
        p.barrier()

    if "G" in phases:
        sgw_d = nI("sgw", [128, 4 * 128])
        sgb_d = nI("sgb", [128, 4])
        slg_d = nI("slg", [128, 512]); slb_d = nI("slb", [128, 512])
        ar.reset()
        sgw = ar.alloc([4, 128], BF16); sgb = ar.alloc([4], F32)
        slg = ar.alloc([512], F32); slb = ar.alloc([512], F32)
        p.dma("pool", sgw, sgw_d.rearrange("p (g t) -> p g t", g=4), writes=["sgw"])
        p.dma("sp", sgb, sgb_d, writes=["sgb"]); p.dma("sp", slg, slg_d, writes=["slg"]); p.dma("sp", slb, slb_d, writes=["slb"])
        epsT = ar.alloc([1], F32)
        p.op("pool", lambda e: e.memset(epsT, 1e-5), writes=["eps"])
        NB = 4
        g_lists = []
        inb = [ar.alloc([2048], F32) for _ in range(NB)]
        t1 = [ar.alloc([1536], F32) for _ in range(NB)]
        gl_ = [ar.alloc([1536], F32) for _ in range(NB)]
        vn = [ar.alloc([512], BF16) for _ in range(NB)]
        st = [ar.alloc([2, 6], F32) for _ in range(NB)]
        mv = [ar.alloc([2], F32) for _ in range(NB)]
        rs = [ar.alloc([1], F32) for _ in range(NB)]
        sl = [ar.alloc([512], F32) for _ in range(NB)]
        oa = [ar.alloc([512], F32) for _ in range(NB)]
        ob = [ar.alloc([512], BF16) for _ in range(NB)]
        for c in range(S // 128):
            b = c % NB
            p.capture()
            K = lambda n: "%s%d" % (n, b)
            X = inb[b]; T1 = t1[b]; GL = gl_[b]
            p.dma("sp", X, tm[c * 128:(c + 1) * 128, 0:2048], writes=[K("inb")])
            uv = X[:, 0:1536]
            p.op("dve", lambda e, T1=T1, uv=uv: e.tensor_tensor(out=T1, in0=uv, in1=uv, op=ALU.mult), reads=[K("inb")], writes=[K("t1")])
            p.op("dve", lambda e, T1=T1: e.tensor_scalar(out=T1, in0=T1, scalar1=0.044715, scalar2=1.0, op0=ALU.mult, op1=ALU.add), reads=[K("t1")], writes=[K("t1")])
            p.op("pool", lambda e, T1=T1, uv=uv: e.tensor_tensor(out=T1, in0=T1, in1=uv, op=ALU.mult), reads=[K("t1"), K("inb")], writes=[K("t1")])
            p.op("act", lambda e, T1=T1: e.activation(out=T1, in_=T1, func=AF.Sigmoid, scale=GC), reads=[K("t1")], writes=[K("t1")])
            p.op("pool", lambda e, T1=T1, uv=uv, GL=GL: e.tensor_tensor(out=GL, in0=T1, in1=uv, op=ALU.mult), reads=[K("t1"), K("inb")], writes=[K("gl")])
            S_ = st[b]; M_ = mv[b]; R_ = rs[b]

            def stats(e, S_=S_, GL=GL):
                for q in range(2):
                    ins = e.bn_stats(out=S_[:, q, :], in_=GL[:, 512 + q * 512:512 + (q + 1) * 512])
                return ins
            p.op("dve", stats, reads=[K("gl")], writes=[K("st")])
            p.op("dve", lambda e, S_=S_, M_=M_: e.bn_aggr(out=M_, in_=S_.rearrange("p a b -> p (a b)")), reads=[K("st")], writes=[K("mv")])
            p.op("act", lambda e, M_=M_, R_=R_: e.activation(out=R_, in_=M_[:, 1:2], func=AF.Sqrt, bias=epsT, scale=1.0), reads=[K("mv"), "eps"], writes=[K("rs")])
            p.op("dve", lambda e, R_=R_: e.reciprocal(out=R_, in_=R_), reads=[K("rs")], writes=[K("rs")])
            VH = None
            return_vh = lambda: None
            vm = GL[:, 512:1024]
            p.op("dve", lambda e, vm=vm, M_=M_, R_=R_: e.tensor_scalar(out=vm, in0=vm, scalar1=M_[:, 0:1], scalar2=R_, op0=ALU.subtract, op1=ALU.mult),
                 reads=[K("gl"), K("mv"), K("rs")], writes=[K("gl")])
            p.op("pool", lambda e, vm=vm: e.tensor_tensor(out=vm, in0=vm, in1=slg, op=ALU.mult), reads=[K("gl"), "slg"], writes=[K("gl")])
            VN = vn[b]
            p.op("pool", lambda e, vm=vm, VN=VN: e.tensor_tensor(out=VN, in0=vm, in1=slb, op=ALU.add), reads=[K("gl"), "slb"], writes=[K("vn")])
            P_ = ps[c % 4]; pk = "PS%d" % (c % 4)

            def mm(e, P_=P_, VN=VN):
                for g in range(4):
                    ins = e.matmul(P_[:, g * 128:(g + 1) * 128], lhsT=sgw[:, g, :], rhs=VN[:, g * 128:(g + 1) * 128], start=True, stop=True)
                return ins
            p.op("pe", mm, reads=["sgw", K("vn")], writes=[pk])
            SL = sl[b]; OA = oa[b]; OB = ob[b]
            p.op("act", lambda e, SL=SL, X=X: e.activation(out=SL, in_=X[:, 1536:2048], func=AF.Silu), reads=[K("inb")], writes=[K("sl")])

            def fin(e, P_=P_, OA=OA, GL=GL):
                for g in range(4):
                    ins = e.scalar_tensor_tensor(out=OA[:, g * 128:(g + 1) * 128], in0=P_[:, g * 128:(g + 1) * 128], scalar=sgb[:, g:g + 1],
                                                 in1=GL[:, g * 128:(g + 1) * 128], op0=ALU.add, op1=ALU.mult)
                return ins
            p.op("dve", fin, reads=[pk, "sgb", K("gl")], writes=[K("oa")])
            p.op("pool", lambda e, OA=OA, SL=SL, OB=OB: e.tensor_tensor(out=OB, in0=OA, in1=SL, op=ALU.mult), reads=[K("oa"), K("sl")], writes=[K("ob")])
            p.dma("sp", o_out[c * 128:(c + 1) * 128, 0:512], OB, reads=[K("ob")])
            g_lists.append(p.end_capture())
        p.replay(g_lists, 4)
        p.barrier()

    if "L" in phases:
        cw_d = nI("cw", [128, 16]); cb_d = nI("cb", [128, 4])
        lwa_d = nI("lwa", [128, 4 * 128]); lwx_d = nI("lwx", [128, 4 * 128])
        lba_d = nI("lba", [128, 4]); lbx_d = nI("lbx", [128, 4]); lam_d = nI("lam", [128, 4])
        identf_d = nI("identf", [128, 128])
        ar.reset()
        cw = ar.alloc([4, 4], F32); cb = ar.alloc([4], F32)
        lwa = ar.alloc([4, 128], BF16); lwx = ar.alloc([4, 128], BF16)
        lba = ar.alloc([4], F32); lbx = ar.alloc([4], F32); lam = ar.alloc([4], F32); c8 = ar.alloc([4], F32)
        identf = ar.alloc([128], F32)
        p.dma("sp", cw, cw_d.rearrange("p (c k) -> p c k", c=4), writes=["cw"]); p.dma("sp", cb, cb_d, writes=["cb"])
        p.dma("pool", lwa, lwa_d.rearrange("p (c n) -> p c n", c=4), writes=["lwa"])
        p.dma("pool", lwx, lwx_d.rearrange("p (c n) -> p c n", c=4), writes=["lwx"])
        p.dma("sp", lba, lba_d, writes=["lba"]); p.dma("sp", lbx, lbx_d, writes=["lbx"]); p.dma("sp", lam, lam_d, writes=["lam"])
        p.dma("sp", identf, identf_d, writes=["identf"])
        one = ar.alloc([1], F32)
        p.op("pool", lambda e: e.memset(one, 1.0), writes=["one"])
        p.op("act", lambda e: e.activation(out=c8, in_=lam, func=AF.Exp, scale=-1.0), reads=["lam"], writes=["c8"])
        p.op("act", lambda e: e.activation(out=c8, in_=c8, func=AF.Ln, bias=one, scale=1.0), reads=["c8", "one"], writes=["c8"])
        p.op("dve", lambda e: e.tensor_scalar(out=c8, in0=c8, scalar1=-8.0, scalar2=None, op0=ALU.mult), reads=["c8"], writes=["c8"])
        carry = ar.alloc([4], F32)
        p.op("pool", lambda e: e.memset(carry, 0.0), writes=["carry0", "carry1", "carry2", "carry3"])
        NB = 4
        l_lists = []
        xin = [ar.alloc([515], F32) for _ in range(NB)]
        gin = [ar.alloc([512], F32) for _ in range(NB)]
        xc = [ar.alloc([512], F32) for _ in range(NB)]
        xcb = [ar.alloc([512], BF16) for _ in range(NB)]
        rr = [ar.alloc([512], F32) for _ in range(NB)]
        ii = [ar.alloc([512], F32) for _ in range(NB)]
        aa = [ar.alloc([512], F32) for _ in range(NB)]
        ss = [ar.alloc([512], F32) for _ in range(NB)]
        hh = [ar.alloc([512], F32) for _ in range(NB)]
        otile = [ar.alloc([4, 512], BF16) for _ in range(2)]
        it = 0
        for Tq in range(S // 512):
            OT = otile[Tq % 2]; otk = "otile%d" % (Tq % 2)
            for ct in range(4):
                b = it % NB
                p.capture()
                K = lambda n: "%s%d" % (n, b)
                XI = xin[b]; GI = gin[b]; XC = xc[b]; XB = xcb[b]; RR = rr[b]; II = ii[b]; AA = aa[b]; SS = ss[b]; HH = hh[b]
                if Tq == 0:
                    p.op("pool", lambda e, XI=XI: e.memset(XI[:, 0:3], 0.0), writes=[K("xin")])
                    p.dma("sp", XI[:, 3:515], fmf[ct, :, 0:512], writes=[K("xin")])
                else:
                    p.dma("sp", XI, fmf[ct, :, Tq * 512 - 3:(Tq + 1) * 512], writes=[K("xin")])
                p.dma("sp", GI, fmf[4 + ct, :, Tq * 512:(Tq + 1) * 512], writes=[K("gin")])
                p.op("dve", lambda e, XC=XC, XI=XI, ct=ct: e.tensor_scalar(out=XC, in0=XI[:, 3:515], scalar1=cw[:, ct, 3:4], scalar2=cb[:, ct:ct + 1],
                                                                         op0=ALU.mult, op1=ALU.add), reads=[K("xin"), "cw", "cb"], writes=[K("xc")])
                for k in range(3):
                    p.op("dve", lambda e, XC=XC, XI=XI, ct=ct, k=k: e.scalar_tensor_tensor(out=XC, in0=XI[:, k:k + 512], scalar=cw[:, ct, k:k + 1], in1=XC,
                                                                                           op0=ALU.mult, op1=ALU.add), reads=[K("xin"), K("xc"), "cw"], writes=[K("xc")])
                p.op("pool", lambda e, XC=XC, XB=XB: e.tensor_copy(out=XB, in_=XC), reads=[K("xc")], writes=[K("xcb")])
                PA = ps[it % 4]; PX = ps[4 + it % 4]; pak = "PS%d" % (it % 4); pxk = "PS%d" % (4 + it % 4)
                p.op("pe", lambda e, PA=PA, XB=XB, ct=ct: e.matmul(PA, lhsT=lwa[:, ct, :], rhs=XB, start=True, stop=True), reads=["lwa", K("xcb")], writes=[pak])
                p.op("pe", lambda e, PX=PX, XB=XB, ct=ct: e.matmul(PX, lhsT=lwx[:, ct, :], rhs=XB, start=True, stop=True), reads=["lwx", K("xcb")], writes=[pxk])
                p.op("act", lambda e, RR=RR, PA=PA, ct=ct: e.activation(out=RR, in_=PA, func=AF.Sigmoid, bias=lba[:, ct:ct + 1], scale=1.0), reads=[pak, "lba"], writes=[K("rr")])
                p.op("act", lambda e, II=II, PX=PX, ct=ct: e.activation(out=II, in_=PX, func=AF.Sigmoid, bias=lbx[:, ct:ct + 1], scale=1.0), reads=[pxk, "lbx"], writes=[K("ii")])
                p.op("act", lambda e, AA=AA, RR=RR, ct=ct: e.activation(out=AA, in_=RR, func=AF.Exp, scale=c8[:, ct:ct + 1]), reads=[K("rr"), "c8"], writes=[K("aa")])
                p.op("pool", lambda e, SS=SS, AA=AA: e.tensor_tensor(out=SS, in0=AA, in1=AA, op=ALU.mult), reads=[K("aa")], writes=[K("ss")])
                p.op("act", lambda e, SS=SS: e.activation(out=SS, in_=SS, func=AF.Sqrt, bias=one, scale=-1.0), reads=[K("ss"), "one"], writes=[K("ss")])
                p.op("pool", lambda e, II=II, XC=XC: e.tensor_tensor(out=II, in0=II, in1=XC, op=ALU.mult), reads=[K("ii"), K("xc")], writes=[K("ii")])
                p.op("pool", lambda e, II=II, SS=SS: e.tensor_tensor(out=II, in0=II, in1=SS, op=ALU.mult), reads=[K("ii"), K("ss")], writes=[K("ii")])
                p.op("dve", lambda e, HH=HH, AA=AA, II=II, ct=ct: e.tensor_tensor_scan(out=HH, data0=AA, data1=II, initial=carry[:, ct:ct + 1], op0=ALU.mult, op1=ALU.add),
                     reads=[K("aa"), K("ii"), "carry%d" % ct], writes=[K("hh")])
                p.op("dve", lambda e, HH=HH, ct=ct: e.tensor_copy(out=carry[:, ct:ct + 1], in_=HH[:, 511:512]), reads=[K("hh")], writes=["carry%d" % ct])
                p.op("act", lambda e, GI=GI: e.activation(out=GI, in_=GI, func=AF.Silu), reads=[K("gin")], writes=[K("gin")])
                p.op("pool", lambda e, HH=HH, GI=GI: e.tensor_tensor(out=HH, in0=HH, in1=GI, op=ALU.mult), reads=[K("hh"), K("gin")], writes=[K("hh")])
                PT = PA; ptk = pak

                def tr(e, PT=PT, HH=HH):
                    for q in range(4):
                        ins = e.transpose(out=PT[:, q * 128:(q + 1) * 128], in_=HH[:, q * 128:(q + 1) * 128], identity=identf)
                    return ins
                p.op("pe", tr, reads=[K("hh"), "identf"], writes=[ptk])
                evac(OT[:, :, ct * 128:(ct + 1) * 128], PT.rearrange("p (q n) -> p q n", q=4), [ptk], [otk])
                it += 1
                if ct == 3:
                    p.dma("sp", o_out[Tq * 512:(Tq + 1) * 512, 512:1024].rearrange("(q p) n -> p q n", p=128), OT, reads=[otk])
                l_lists.append(p.end_capture())
        p.replay(l_lists, 4)
        p.barrier()

    if "M" in phases:
        memT_d = nI("memT", [128, 16 * 256])
        wmk_d = nI("wmk", [128, 16 * 512]); wmv_d = nI("wmv", [128, 16 * 512])
        ar.reset()
        memT = ar.alloc([16, 256], BF16); wmk = ar.alloc([16, 512], BF16); wmv = ar.alloc([16, 512], BF16)
        p.dma("pool", memT, memT_d.rearrange("p (k n) -> p k n", k=16), writes=["memT"])
        p.dma("pool", wmk, wmk_d.rearrange("p (k n) -> p k n", k=16), writes=["wmk"])
        p.dma("pool", wmv, wmv_d.rearrange("p (k n) -> p k n", k=16), writes=["wmv"])
        KT = ar.alloc([4, 256], BF16)
        Va = ar.alloc([2, 2, 257], BF16)
        p.op("pool", lambda e: e.memset(Va[:, :, :, 256:257], 1.0), writes=["Va"])
        for tl in range(4):
            P_ = ps[tl % 2]; pk = "PS%d" % (tl % 2)

            def mm(e, P_=P_, tl=tl):
                for kc in range(16):
                    ins = e.matmul(P_[:, 0:256], lhsT=wmk[:, kc, tl * 128:(tl + 1) * 128], rhs=memT[:, kc, :], start=(kc == 0), stop=(kc == 15))
                return ins
            p.op("pe", mm, reads=["wmk", "memT"], writes=[pk])
            evac(KT[:, tl, :], P_[:, 0:256], [pk], ["KT"])
        for mt in range(2):
            P_ = ps[mt % 2]; pk = "PS%d" % (mt % 2)

            def mm(e, P_=P_, mt=mt):
                for kc in range(16):
                    ins = e.matmul(P_, lhsT=memT[:, kc, mt * 128:(mt + 1) * 128], rhs=wmv[:, kc, :], start=(kc == 0), stop=(kc == 15))
                return ins
            p.op("pe", mm, reads=["wmv", "memT"], writes=[pk])
            evac(Va[:, mt, :, 0:256], P_.rearrange("p (h d) -> p h d", h=2), [pk], ["Va"])
        import os
        MSTOP = int(os.environ.get('MSTOP', '9'))
        NB = 2
        qT = [ar.alloc([4, 512], BF16) for _ in range(NB)]
        gm = [ar.alloc([4, 512], F32) for _ in range(NB)]
        PT_ = [ar.alloc([2, 512], BF16) for _ in range(2)]
        rz = [ar.alloc([1], F32) for _ in range(4)]
        om = [ar.alloc([4, 512], F32) for _ in range(NB)]
        omb = [ar.alloc([4, 512], BF16) for _ in range(NB)]
        it = 0
        for Tq in range(S // 512 if MSTOP > 1 else 0):
            b = Tq % NB
            K = lambda n: "%s%d" % (n, b)
            QT = qT[b]; GM = gm[b]; OM = om[b]; OMB = omb[b]
            p.dma("sp", QT, fmb[8:12, :, Tq * 512:(Tq + 1) * 512].rearrange("t p n -> p t n"), writes=[K("qT")])
            p.dma("sp", GM, tm[Tq * 512:(Tq + 1) * 512, 5 * 512:6 * 512].rearrange("(q p) n -> p q n", p=128), writes=[K("gm")])
            p.op("act", lambda e, GM=GM: e.activation(out=GM, in_=GM, func=AF.Silu), reads=[K("gm")], writes=[K("gm")])
            for hd in range(2 if MSTOP > 2 else 0):
                PTb = PT_[it % 2]; ptk = "PT%d" % (it % 2)
                for mt in range(2):
                    P_ = ps[(2 * it + mt) % 4]; pk = "PS%d" % ((2 * it + mt) % 4)

                    def mm(e, P_=P_, hd=hd, mt=mt, QT=QT):
                        for dc in range(2):
                            ins = e.matmul(P_, lhsT=KT[:, 2 * hd + dc, mt * 128:(mt + 1) * 128], rhs=QT[:, 2 * hd + dc, :], start=(dc == 0), stop=(dc == 1))
                        return ins
                    p.op("pe", mm, reads=["KT", K("qT")], writes=[pk])
                    p.op("act", lambda e, P_=P_, PTb=PTb, mt=mt: e.activation(out=PTb[:, mt, :], in_=P_, func=AF.Exp), reads=[pk], writes=[ptk])
                for qs in range(4 if MSTOP > 3 else 0):
                    PO = ps[4 + (it * 4 + qs) % 4]; pok = "PS%d" % (4 + (it * 4 + qs) % 4)

                    def mm2(e, PO=PO, PTb=PTb, qs=qs, hd=hd):
                        for mt in range(2):
                            ins = e.matmul(PO[:, 0:257], lhsT=PTb[:, mt, qs * 128:(qs + 1) * 128], rhs=Va[:, mt, hd, :], start=(mt == 0), stop=(mt == 1))
                        return ins
                    p.op("pe", mm2, reads=[ptk, "Va"], writes=[pok])
                    RZ = rz[(it * 4 + qs) % 4]; rzk = "rz%d" % ((it * 4 + qs) % 4)
                    p.op("dve", lambda e, RZ=RZ, PO=PO: e.reciprocal(out=RZ, in_=PO[:, 256:257]), reads=[pok], writes=[rzk])
                    p.op("dve", lambda e, RZ=RZ, PO=PO, OM=OM, GM=GM, qs=qs, hd=hd: e.scalar_tensor_tensor(
                        out=OM[:, qs, hd * 256:(hd + 1) * 256], in0=PO[:, 0:256], scalar=RZ, in1=GM[:, qs, hd * 256:(hd + 1) * 256],
                        op0=ALU.mult, op1=ALU.mult), reads=[pok, rzk, K("gm")], writes=[K("om")])
                it += 1
            p.op("pool", lambda e, OM=OM, OMB=OMB: e.tensor_copy(out=OMB, in_=OM), reads=[K("om")], writes=[K("omb")])
            p.dma("sp", o_out[Tq * 512:(Tq + 1) * 512, 1536:2048].rearrange("(q p) n -> p q n", p=128), OMB, reads=[K("omb")])
        p.barrier()
    if "N" in phases:
        build_nsa(nc, p, ar, ps, o_out, fmb, tm, sfx)
    return o_out


S = 4096
WSEL = 1920
WWIN = 1152
WCMP = 3392
NEG = -30000.0


def build_nsa(nc, p, ar, ps, o_out, fmb, tm, sfx=""):
    nI = lambda n, s, d=F32: nc.dram_tensor(n + sfx, s, d, kind="ExternalInput").ap()
    w1k_d = nI("w1k", [128, 32 * 256]); w1v_d = nI("w1v", [128, 32 * 256])
    pek_d = nI("pek", [128, 32]); pev_d = nI("pev", [128, 32])
    w2kp_d = nI("w2kp", [128, 2 * 2 * 128]); w2v_d = nI("w2v", [128, 2 * 64])
    gateb_d = nI("gateb", [128, 24]); ovl_d = nI("ovl", [128, 2 * 64])
    hsel_d = nI("hsel", [8, 128, WSEL]); hwin_d = nI("hwin", [8, 128, WWIN]); hcmp_d = nI("hcmp", [8, 128, WCMP])
    addc_d = nI("addc", [128, 32 * 64]); emat_d = nI("emat", [64, S])
    b31_d = nI("b31", [128, 8])
    identb_d = nI("identn", [128, 128]); identf_d = nI("identnf", [128, 128])
    kcc_s = nc.dram_tensor("kcc_s" + sfx, [128, 256], BF16).ap()
    vc_s = nc.dram_tensor("vc_s" + sfx, [128, 2 * 2 * 129], BF16).ap()
    PK = lambda i: "PS%d" % i
    cnt = {"ev": 0}

    def evac(out, in_, reads, writes):
        cnt["ev"] += 1
        if cnt["ev"] % 2:
            p.op("act", lambda e: e.copy(out=out, in_=in_), reads=reads, writes=writes)
        else:
            p.op("dve", lambda e: e.tensor_copy(out=out, in_=in_), reads=reads, writes=writes)

    ar.reset()
    w1 = [ar.alloc([32, 256], BF16) for _ in range(2)]
    pe = [ar.alloc([32], BF16) for _ in range(2)]
    p.dma("pool", w1[0], w1k_d.rearrange("p (t h) -> p t h", t=32), writes=["w1k"])
    p.dma("pool", w1[1], w1v_d.rearrange("p (t h) -> p t h", t=32), writes=["w1v"])
    p.dma("pool", pe[0], pek_d, writes=["pek"]); p.dma("pool", pe[1], pev_d, writes=["pev"])
    w2kp = ar.alloc([2, 2, 128], BF16); w2v = ar.alloc([2, 64], BF16); ovl = ar.alloc([2, 64], F32)
    p.dma("pool", w2kp, w2kp_d.rearrange("p (a b c) -> p a b c", a=2, b=2), writes=["w2kp"])
    p.dma("pool", w2v, w2v_d.rearrange("p (a c) -> p a c", a=2), writes=["w2v"])
    p.dma("sp", ovl, ovl_d.rearrange("p (a c) -> p a c", a=2), writes=["ovl"])
    xcT = [ar.alloc([256, 16], BF16) for _ in range(2)]
    p.dma("sp", xcT[0].rearrange("p a b -> p (a b)"), fmb[4], writes=["kcT"])
    p.dma("sp", xcT[1].rearrange("p a b -> p (a b)"), fmb[5], writes=["vcT"])
    c1 = ar.alloc([2, 2], F32)
    hT = [ar.alloc([2, 2, 256], BF16) for _ in range(2)]
    kcc = ar.alloc([256], BF16)
    VC = ar.alloc([2, 2, 129], BF16)
    p.op("pool", lambda e: e.memset(VC, 0.0), writes=["VC"])
    p.op("pool", lambda e: e.memset(kcc, 0.0), writes=["kcc"])
    for wi, (wk, pk_) in enumerate((("w1k", "pek"), ("w1v", "pev"))):
        for ht in range(2):
            P_ = ps[ht]

            def mm(e, P_=P_, wi=wi, ht=ht):
                for t in range(32):
                    ins = e.matmul(P_[:, 0:1], lhsT=w1[wi][0:64, t, ht * 128:(ht + 1) * 128], rhs=pe[wi][0:64, t:t + 1], start=(t == 0), stop=(t == 31))
                return ins
            p.op("pe", mm, reads=[wk, pk_], writes=[PK(ht)])
            p.op("dve", lambda e, P_=P_, wi=wi, ht=ht: e.tensor_copy(out=c1[:, wi, ht:ht + 1], in_=P_[:, 0:1]), reads=[PK(ht)], writes=["c1"])
    it = 0
    for wi, (wk, xk) in enumerate((("w1k", "kcT"), ("w1v", "vcT"))):
        for kvl in range(2):
            for ht in range(2):
                P_ = ps[2 + it % 2]; pk = PK(2 + it % 2)

                def mm(e, P_=P_, wi=wi, kvl=kvl, ht=ht):
                    for t in range(32):
                        q, r = divmod(t, 16)
                        ins = e.matmul(P_[:, 0:255], lhsT=w1[wi][kvl * 64:(kvl + 1) * 64, t, ht * 128:(ht + 1) * 128],
                                       rhs=xcT[wi][kvl * 64:(kvl + 1) * 64, q:q + 255, r], start=(t == 0), stop=(t == 31))
                    return ins
                p.op("pe", mm, reads=[wk, xk], writes=[pk])
                p.op("act", lambda e, P_=P_, wi=wi, kvl=kvl, ht=ht: e.activation(out=hT[wi][:, kvl, ht, 0:255], in_=P_[:, 0:255], func=AF.Silu,
                                                                               bias=c1[:, wi, ht:ht + 1], scale=1.0), reads=[pk, "c1"], writes=["hT%d" % wi])
                it += 1
    P_ = ps[4]

    def mmk(e):
        i = 0
        for kvl in range(2):
            for hc in range(2):
                ins = e.matmul(P_[:, 0:255], lhsT=w2kp[:, kvl, hc, :], rhs=hT[0][:, kvl, hc, 0:255], start=(i == 0), stop=(i == 3))
                i += 1
        return ins
    p.op("pe", mmk, reads=["w2kp", "hT0"], writes=[PK(4)])
    p.op("act", lambda e: e.copy(out=kcc[:, 0:255], in_=P_[:, 0:255]), reads=[PK(4)], writes=["kcc"])
    for ec in range(2):
        nn = 128 if ec == 0 else 127
        p.op("dve", lambda e, ec=ec, nn=nn: e.memset(VC[0:nn, ec, :, 64:65], 1.0), reads=["VC"], writes=["VC"])
        for kvl in range(2):
            P2 = ps[5 + (ec * 2 + kvl) % 2]; pk = PK(5 + (ec * 2 + kvl) % 2)

            def mmv(e, P2=P2, ec=ec, kvl=kvl, nn=nn):
                for hc in range(2):
                    ins = e.matmul(P2[0:nn, 0:64], lhsT=hT[1][:, kvl, hc, ec * 128:ec * 128 + nn], rhs=w2v[:, hc, :], start=(hc == 0), stop=(hc == 1))
                return ins
            p.op("pe", mmv, reads=["w2v", "hT1"], writes=[pk])
            p.op("dve", lambda e, P2=P2, ec=ec, kvl=kvl, nn=nn: e.tensor_copy(out=VC[0:nn, ec, kvl, 65:129], in_=P2[0:nn, 0:64]), reads=[pk, "VC"], writes=["VC"])
            p.op("dve", lambda e, ec=ec, kvl=kvl, nn=nn: e.tensor_copy(out=VC[0:nn, ec, kvl, 0:64], in_=ovl[0:nn, ec, :]), reads=["ovl", "VC"], writes=["VC"])
    p.dma("sp", kcc_s, kcc, reads=["kcc"])
    p.dma("sp", vc_s, VC.rearrange("p a b c -> p (a b c)"), reads=["VC"])
    p.barrier()

    import os
    NSTOP = int(os.environ.get('NSTOP', '9'))
    if NSTOP == 0:
        return
    ar.reset()
    identb = ar.alloc([128], BF16); identf = ar.alloc([128], F32)
    p.dma("pool", identb, identb_d, writes=["identb"]); p.dma("sp", identf, identf_d, writes=["identf"])
    kccN = ar.alloc([256], BF16); VCN = ar.alloc([2, 2, 129], BF16)
    p.dma("sp", kccN, kcc_s, writes=["kcc"]); p.dma("sp", VCN.rearrange("p a b c -> p (a b c)"), vc_s, writes=["VC"])
    ksT = ar.alloc([S], BF16); kwT = ar.alloc([S], BF16)
    p.dma("sp", ksT, fmb[6], writes=["ksT"]); p.dma("sp", kwT, fmb[7], writes=["kwT"])
    VS = ar.alloc([32, 2, 65], BF16); VW = ar.alloc([32, 2, 65], BF16)
    p.op("pool", lambda e: e.memset(VS[:, :, :, 64:65], 1.0), writes=["VS"])
    p.op("pool", lambda e: e.memset(VW[:, :, :, 64:65], 1.0), writes=["VW"])
    for kvl in range(2):
        p.dma("pool", VS[:, :, kvl, 0:64], tm[:, 3072 + kvl * 64:3072 + kvl * 64 + 64].rearrange("(t p) c -> p t c", p=128), reads=["VS"], writes=["VS"])
        p.dma("pool", VW[:, :, kvl, 0:64], tm[:, 3200 + kvl * 64:3200 + kvl * 64 + 64].rearrange("(t p) c -> p t c", p=128), reads=["VW"], writes=["VW"])
    hsel = ar.alloc([8, WSEL], BF16); hwin = ar.alloc([8, WWIN], BF16)
    for hl in range(8):
        p.dma("pool", hsel[:, hl, :], hsel_d[hl], writes=["hsel"])
        p.dma("pool", hwin[:, hl, :], hwin_d[hl], writes=["hwin"])
    emat = ar.alloc([S], BF16)
    p.op("pool", lambda e: e.memset(emat[64:128, :], 0.0), writes=["emat"])
    p.dma("pool", emat[0:64, :], emat_d, reads=["emat"], writes=["emat"])
    addc = ar.alloc([32, 64], F32)
    p.dma("sp", addc, addc_d.rearrange("p (a c) -> p a c", a=32), writes=["addc"])
    gateb = ar.alloc([24], F32)
    b31 = ar.alloc([8], F32)
    p.dma("sp", b31, b31_d, writes=["b31"])
    p.dma("sp", gateb, gateb_d, writes=["gateb"])
    NB = 2
    qT = [ar.alloc([4, 512], BF16) for _ in range(NB)]
    gt = [ar.alloc([4, 24], F32) for _ in range(NB)]
    gc = [ar.alloc([4, 512], F32) for _ in range(NB)]
    hc_ = [ar.alloc([8, 2, 512], BF16) for _ in range(NB)]
    imp = ar.alloc([4, 2, 64], F32)
    selb = ar.alloc([4, 2, 64], F32)
    m8 = ar.alloc([4, 2, 8], F32)
    selT = ar.alloc([2, 512], BF16)
    p.op("pool", lambda e: e.memset(selT, 0.0), writes=["selT"])
    PT = [ar.alloc([512], BF16) for _ in range(4)]
    oacc = [ar.alloc([4, 512], F32) for _ in range(NB)]
    oab = [ar.alloc([4, 512], BF16) for _ in range(NB)]
    rz = [ar.alloc([1], F32) for _ in range(4)]
    rg = [ar.alloc([1], F32) for _ in range(4)]
    rz4 = [ar.alloc([4], F32) for _ in range(2)]
    rg4 = [ar.alloc([4], F32) for _ in range(2)]
    tmp4 = [ar.alloc([4, 64], F32) for _ in range(2)]
    sidx = {"s": 0, "pt": 0, "r": 0, "h": 0}

    def s_bank():
        i = sidx["s"] % 4
        sidx["s"] += 1
        return ps[i], PK(i)

    def pt_buf():
        i = sidx["pt"] % 4
        sidx["pt"] += 1
        return PT[i], "PT%d" % i

    def load_q(Q):
        b = Q % NB
        K = lambda n: "%s%d" % (n, b)
        QT = qT[b]; GT = gt[b]; GC = gc[b]; HC = hc_[b]
        nec = 2 if Q >= 4 else 1
        p.dma("sp", QT, fmb[0:4, :, Q * 512:(Q + 1) * 512].rearrange("t p n -> p t n"), writes=[K("qT")])
        p.dma("sp", GT, tm[Q * 512:(Q + 1) * 512, 3328:3352].rearrange("(q p) n -> p q n", p=128), writes=[K("gt")])
        p.dma("sp", GC, tm[Q * 512:(Q + 1) * 512, 4 * 512:5 * 512].rearrange("(q p) n -> p q n", p=128), writes=[K("gc")])
        for hl in range(8):
            for ec in range(nec):
                c0 = min(512 * Q - 2048 * ec, WCMP - 512)
                p.dma("pool", HC[:, hl, ec, :], hcmp_d[hl, :, c0:c0 + 512], writes=[K("hc")])

        def gadd(e, GT=GT):
            for qs in range(4):
                ins = e.tensor_tensor(out=GT[:, qs, :], in0=GT[:, qs, :], in1=gateb, op=ALU.add)
            return ins
        p.op("dve", gadd, reads=[K("gt"), "gateb"], writes=[K("gt")])
        p.op("act", lambda e, GT=GT: e.activation(out=GT, in_=GT, func=AF.Sigmoid), reads=[K("gt")], writes=[K("gt")])
        p.op("act", lambda e, GC=GC: e.activation(out=GC, in_=GC, func=AF.Silu), reads=[K("gc")], writes=[K("gc")])

    NQ = int(os.environ.get('NQ', '8'))
    for Q in range(NQ if NSTOP > 1 else 0):
        b = Q % NB
        K = lambda n: "%s%d" % (n, b)
        QT = qT[b]; GT = gt[b]; GC = gc[b]; HC = hc_[b]; OA = oacc[b]; OB = oab[b]
        nec = 2 if Q >= 4 else 1
        if Q == 0:
            load_q(0)

        def finish_head(PO_list, br, hl, zcol, ocol, first, Kf=K, GT=GT, OA=OA):
            for qs in range(4):
                PO, pok = PO_list[qs]
                ri = sidx["r"] % 4
                sidx["r"] += 1
                RZ = rz[ri]; RG = rg[ri]; rzk = "rz%d" % ri; rgk = "rg%d" % ri
                p.op("dve", lambda e, RZ=RZ, PO=PO: e.tensor_scalar(out=RZ, in0=PO[:, zcol:zcol + 1], scalar1=1e-30, scalar2=None, op0=ALU.max), reads=[pok], writes=[rzk])
                p.op("dve", lambda e, RZ=RZ: e.reciprocal(out=RZ, in_=RZ), reads=[rzk], writes=[rzk])
                p.op("dve", lambda e, RZ=RZ, RG=RG, qs=qs: e.tensor_tensor(out=RG, in0=RZ, in1=GT[:, qs, br * 8 + hl:br * 8 + hl + 1], op=ALU.mult),
                     reads=[rzk, Kf("gt")], writes=[rgk])
                dst = OA[:, qs, hl * 64:(hl + 1) * 64]
                if first:
                    p.op("dve", lambda e, PO=PO, RG=RG, dst=dst: e.tensor_scalar(out=dst, in0=PO[:, ocol:ocol + 64], scalar1=RG, scalar2=None, op0=ALU.mult),
                         reads=[pok, rgk], writes=[Kf("oacc")])
                else:
                    p.op("dve", lambda e, PO=PO, RG=RG, dst=dst: e.scalar_tensor_tensor(out=dst, in0=PO[:, ocol:ocol + 64], scalar=RG, in1=dst, op0=ALU.mult, op1=ALU.add),
                         reads=[pok, rgk, Kf("oacc")], writes=[Kf("oacc")])
                if br == 0:
                    kvl, g = divmod(hl, 4)
                    idst = imp[:, qs, kvl, :]
                    if g == 0:
                        p.op("dve", lambda e, PO=PO, RZ=RZ, idst=idst: e.tensor_scalar(out=idst, in0=PO[:, 0:64], scalar1=RZ, scalar2=None, op0=ALU.mult),
                             reads=[pok, rzk], writes=["imp"])
                    else:
                        p.op("dve", lambda e, PO=PO, RZ=RZ, idst=idst: e.scalar_tensor_tensor(out=idst, in0=PO[:, 0:64], scalar=RZ, in1=idst, op0=ALU.mult, op1=ALU.add),
                             reads=[pok, rzk, "imp"], writes=["imp"])

        def finish_packed(POb, pok, br, hl, Kf=K, GT=GT, OA=OA):
            i2 = sidx["r"] % 2
            sidx["r"] += 1
            RZ = rz4[i2]; RG = rg4[i2]; TM = tmp4[i2]
            rzk = "rz4%d" % i2; rgk = "rg4%d" % i2; tmk = "tmp4%d" % i2
            pv = POb[:, 0:260].rearrange("p (q c) -> p q c", c=65)
            p.op("dve", lambda e: e.tensor_scalar(out=RZ, in0=pv[:, :, 64], scalar1=1e-30, scalar2=None, op0=ALU.max), reads=[pok], writes=[rzk])
            p.op("dve", lambda e: e.reciprocal(out=RZ, in_=RZ), reads=[rzk], writes=[rzk])
            p.op("dve", lambda e: e.tensor_tensor(out=RG, in0=RZ, in1=GT[:, :, br * 8 + hl], op=ALU.mult), reads=[rzk, Kf("gt")], writes=[rgk])
            p.op("dve", lambda e: e.tensor_tensor(out=TM, in0=pv[:, :, 0:64], in1=RG.unsqueeze(2).broadcast_to([128, 4, 64]), op=ALU.mult),
                 reads=[pok, rgk], writes=[tmk])
            dst = OA[:, :, hl * 64:(hl + 1) * 64]
            p.op("pool", lambda e: e.tensor_tensor(out=dst, in0=dst, in1=TM, op=ALU.add), reads=[tmk, Kf("oacc")], writes=[Kf("oacc")])

        pend = []
        DEPTH_SEL = int(os.environ.get("PDEPTH", "2"))

        def flush(keep=0):
            while len(pend) > keep:
                pend.pop(0)()
        for hl in range(8):
            kvl, g = divmod(hl, 4)
            pr = slice(kvl * 64, (kvl + 1) * 64)
            pts = []
            for ec in range(nec):
                SP, spk = s_bank()
                PTb, ptk = pt_buf()

                def mm(e, SP=SP, ec=ec, hl=hl, g=g, pr=pr, QT=QT, HC=HC):
                    e.matmul(SP, lhsT=kccN[pr, ec * 128:(ec + 1) * 128], rhs=QT[pr, g, :], start=True, stop=False)
                    return e.matmul(SP, lhsT=identb, rhs=HC[:, hl, ec, :], start=False, stop=True)
                p.op("pe", mm, reads=["kcc", K("qT"), K("hc"), "identb"], writes=[spk])
                p.op("act", lambda e, SP=SP, PTb=PTb: e.activation(out=PTb, in_=SP, func=AF.Exp), reads=[spk], writes=[ptk])
                pts.append((PTb, ptk))
            flush()

            def second(hl=hl, kvl=kvl, g=g, pts=pts):
                hb = 4 + 2 * (hl % 2)
                for half in range(2):
                    POb = ps[hb + half]; pok = PK(hb + half)

                    def mm2(e, POb=POb, half=half, kvl=kvl, pts=pts):
                        n_ = 0
                        for q2 in range(2):
                            qs = half * 2 + q2
                            for ec in range(len(pts)):
                                ins = e.matmul(POb[:, q2 * 129:(q2 + 1) * 129], lhsT=pts[ec][0][:, qs * 128:(qs + 1) * 128], rhs=VCN[:, ec, kvl, :],
                                               start=(n_ == 0), stop=(ec == len(pts) - 1))
                                n_ += 1
                        return ins
                    p.op("pe", mm2, reads=[k for (_, k) in pts] + ["VC"], writes=[pok])
                    i2 = sidx["r"] % 2
                    sidx["r"] += 1
                    RZ = rz4[i2][:, 0:2]; RG = rg4[i2][:, 0:2]; TM = tmp4[i2][:, 0:2, :]
                    rzk = "rz4%d" % i2; rgk = "rg4%d" % i2; tmk = "tmp4%d" % i2
                    pv = POb[:, 0:258].rearrange("p (q c) -> p q c", c=129)
                    qsl = slice(half * 2, half * 2 + 2)
                    p.op("dve", lambda e, RZ=RZ, pv=pv: e.tensor_scalar(out=RZ, in0=pv[:, :, 64], scalar1=1e-30, scalar2=None, op0=ALU.max), reads=[pok], writes=[rzk])
                    p.op("dve", lambda e, RZ=RZ: e.reciprocal(out=RZ, in_=RZ), reads=[rzk], writes=[rzk])
                    p.op("dve", lambda e, RZ=RZ, RG=RG, qsl=qsl, hl=hl, GT=GT: e.tensor_tensor(out=RG, in0=RZ, in1=GT[:, qsl, hl], op=ALU.mult), reads=[rzk, K("gt")], writes=[rgk])
                    dst = OA[:, qsl, hl * 64:(hl + 1) * 64]
                    p.op("dve", lambda e, pv=pv, RG=RG, dst=dst: e.tensor_tensor(out=dst, in0=pv[:, :, 65:129], in1=RG.unsqueeze(2).broadcast_to([128, 2, 64]), op=ALU.mult),
                         reads=[pok, rgk], writes=[K("oacc")])
                    idst = imp[:, qsl, kvl, :]
                    if g == 0:
                        p.op("dve", lambda e, pv=pv, RZ=RZ, idst=idst: e.tensor_tensor(out=idst, in0=pv[:, :, 0:64], in1=RZ.unsqueeze(2).broadcast_to([128, 2, 64]), op=ALU.mult),
                             reads=[pok, rzk], writes=["imp"])
                    else:
                        p.op("dve", lambda e, pv=pv, RZ=RZ, TM=TM: e.tensor_tensor(out=TM, in0=pv[:, :, 0:64], in1=RZ.unsqueeze(2).broadcast_to([128, 2, 64]), op=ALU.mult),
                             reads=[pok, rzk], writes=[tmk])
                        p.op("pool", lambda e, TM=TM, idst=idst: e.tensor_tensor(out=idst, in0=idst, in1=TM, op=ALU.add), reads=[tmk, "imp"], writes=["imp"])
            pend.append(second)
        flush()
        if Q + 1 < NQ:
            load_q(Q + 1)
        for kvl in range(2 if NSTOP > 2 else 0):
            for qs in range(4):
                I_ = imp[:, qs, kvl, :]; SB = selb[:, qs, kvl, :]; M8 = m8[:, qs, kvl, :]
                p.op("dve", lambda e, I_=I_, qs=qs, Q=Q: e.tensor_tensor(out=I_, in0=I_, in1=addc[:, 4 * Q + qs, :], op=ALU.add), reads=["imp", "addc"], writes=["imp"])
                p.op("dve", lambda e, I_=I_, M8=M8: e.max(out=M8, in_=I_), reads=["imp"], writes=["m8"])
                p.op("dve", lambda e, I_=I_, M8=M8, SB=SB: e.tensor_scalar(out=SB, in0=I_, scalar1=M8[:, 7:8], scalar2=None, op0=ALU.is_ge), reads=["imp", "m8"], writes=["selb"])
                p.op("dve", lambda e, SB=SB: e.tensor_scalar(out=SB, in0=SB, scalar1=-NEG, scalar2=NEG, op0=ALU.mult, op1=ALU.add), reads=["selb"], writes=["selb"])
                SP, spk = s_bank()
                p.op("pe", lambda e, SP=SP, SB=SB: e.transpose(out=SP[0:64, 0:128], in_=SB, identity=identf), reads=["selb", "identf"], writes=[spk])
                p.op("act", lambda e, SP=SP, kvl=kvl, qs=qs: e.copy(out=selT[0:64, kvl, qs * 128:(qs + 1) * 128], in_=SP[0:64, 0:128]), reads=[spk], writes=["selT"])
        for br in ({4: (1,), 5: (2,)}.get(NSTOP, (2, 1)) if NSTOP > 3 else ()):
            kT = ksT if br == 1 else kwT
            kTk = "ksT" if br == 1 else "kwT"
            Vb = VS if br == 1 else VW
            Vk = "VS" if br == 1 else "VW"
            H = hsel if br == 1 else hwin
            Hk = "hsel" if br == 1 else "hwin"
            t_lo = 0 if br == 1 else max(0, 4 * Q - 2)
            t_hi = 4 * Q + 3
            for hl in range(8):
                kvl, g = divmod(hl, 4)
                pr = slice(kvl * 64, (kvl + 1) * 64)
                hb = 4 + sidx["h"] % 2
                sidx["h"] += 1
                POb = ps[hb]; pobk = PK(hb)
                for t in range(t_lo, t_hi + 1):
                    SP, spk = s_bank()
                    PTb, ptk = pt_buf()
                    o = 512 * Q - 128 * t
                    c0 = (min(o, 1024) if br == 1 else o) + 384

                    far = (br == 1 and o >= 1024)

                    def mm(e, SP=SP, t=t, hl=hl, g=g, pr=pr, QT=QT, c0=c0, kT=kT, H=H, br=br, kvl=kvl, far=far):
                        e.matmul(SP, lhsT=kT[pr, t * 128:(t + 1) * 128], rhs=QT[pr, g, :], start=True, stop=False)
                        if br == 1:
                            ins = e.matmul(SP, lhsT=emat[:, t * 128:(t + 1) * 128], rhs=selT[:, kvl, :], start=False, stop=far)
                            if far:
                                return ins
                        return e.matmul(SP, lhsT=identb, rhs=H[:, hl, c0:c0 + 512], start=False, stop=True)
                    p.op("pe", mm, reads=[kTk, K("qT"), Hk, "identb", "emat", "selT"], writes=[spk])
                    if far:
                        p.op("act", lambda e, SP=SP, PTb=PTb, hl=hl: e.activation(out=PTb, in_=SP, func=AF.Exp, bias=b31[:, hl:hl + 1], scale=1.0),
                             reads=[spk, "b31"], writes=[ptk])
                    else:
                        p.op("act", lambda e, SP=SP, PTb=PTb: e.activation(out=PTb, in_=SP, func=AF.Exp), reads=[spk], writes=[ptk])
                    flush(DEPTH_SEL - 1)
                    qss = []
                    for qs in range(4):
                        u = 4 * Q + qs
                        lo = 0 if br == 1 else max(0, u - 2)
                        if lo <= t <= u:
                            qss.append((qs, t == lo, t == u))

                    def second(PTb=PTb, ptk=ptk, t=t, kvl=kvl, qss=qss, Vb=Vb, Vk=Vk, last=(t == t_hi), first_t=(t == t_lo), POb=POb, pobk=pobk, br=br, hl=hl):
                        def mm2(e):
                            for n_, (qs, st_, sp_) in enumerate(qss):
                                ins = e.matmul(POb[:, qs * 65:(qs + 1) * 65], lhsT=PTb[:, qs * 128:(qs + 1) * 128], rhs=Vb[:, t, kvl, :],
                                               start=(first_t and n_ == 0), stop=sp_)
                            return ins
                        p.op("pe", mm2, reads=[ptk, Vk], writes=[pobk])
                        if last:
                            finish_packed(POb, pobk, br, hl)
                    pend.append(second)
            flush()
        p.op("pool", lambda e, OA=OA, GC=GC, OB=OB: e.tensor_tensor(out=OB, in0=OA, in1=GC, op=ALU.mult), reads=[K("oacc"), K("gc")], writes=[K("oab")])
        p.dma("sp", o_out[Q * 512:(Q + 1) * 512, 1024:1536].rearrange("(q p) n -> p q n", p=128), OB, reads=[K("oab")])
    p.barrier()


import numpy as np
D_BR = 1024; KVW = 256
SIZES = (D_BR,)*6 + (KVW,)*6 + (48, D_BR, D_BR, D_BR, 8192)
OFF = np.concatenate([[0], np.cumsum(SIZES)]).astype(int)
(O_U, O_V, O_GA, O_XB, O_GB, O_Q, O_KC, O_VC, O_KS, O_VS, O_KW, O_VW, O_GL, O_GC, O_QM, O_GM, O_MG) = [int(v) for v in OFF[:17]]

def arr_k(w, n):
    K, C = w.shape
    nt = C // n
    a = w.reshape(16, 128, nt, n).transpose(2, 1, 0, 3)
    return np.ascontiguousarray(a).reshape(nt, 128, 16 * n)

def s1_inputs(j, x_b, mem_b, l, W):
    w_in = W["w_in"][l]
    f32 = np.float32
    cols = []
    for g in range(4):
        for kvl in range(2):
            hq = (2 * j + kvl) * 4 + g
            cols.append(np.arange(O_Q + hq * 64, O_Q + hq * 64 + 64))
    for base in (O_KC, O_VC, O_KS, O_KW):
        cols.append(np.arange(base + 2 * j * 64, base + 2 * j * 64 + 128))
    cols.append(np.arange(O_QM + 2 * j * 256, O_QM + 2 * j * 256 + 512))
    cols.append(np.arange(O_XB + j * 512, O_XB + j * 512 + 512))
    cols.append(np.arange(O_GB + j * 512, O_GB + j * 512 + 512))
    fm_cols = np.concatenate(cols)
    assert fm_cols.size == 20 * 128
    wfm = arr_k(w_in[:, fm_cols], 128)
    tmc = [np.arange(O_U + j * 512, O_U + j * 512 + 512),
           np.arange(O_V + j * 512, O_V + j * 512 + 512), np.arange(O_V + (1 - j) * 512, O_V + (1 - j) * 512 + 512),
           np.arange(O_GA + j * 512, O_GA + j * 512 + 512), np.arange(O_GC + j * 512, O_GC + j * 512 + 512),
           np.arange(O_GM + j * 512, O_GM + j * 512 + 512)]
    glc = np.concatenate([np.arange(O_GL + br * 16 + 2 * j * 4, O_GL + br * 16 + 2 * j * 4 + 8) for br in range(3)])
    last = np.concatenate([np.arange(O_VS + 2 * j * 64, O_VS + 2 * j * 64 + 128), np.arange(O_VW + 2 * j * 64, O_VW + 2 * j * 64 + 128), glc])
    wl = np.zeros((2048, 512), f32); wl[:, :280] = w_in[:, last]
    wtm = np.concatenate([arr_k(w_in[:, c], 512) for c in tmc] + [arr_k(wl, 512)], axis=0)
    im = {"x_tok1": (np.ascontiguousarray(x_b) if x_b is not None else None), "wfm": wfm, "wtm": wtm, "ident1": np.eye(128, dtype=f32)}
    sw = W["sgu_w"][l][4 * j:4 * j + 4]
    tri = np.tril(np.ones((128, 128), bool))
    sw = np.where(tri[None], sw, 0).astype(f32)
    im["sgw"] = np.ascontiguousarray(sw.transpose(2, 0, 1)).reshape(128, 512)
    im["sgb"] = np.ascontiguousarray(W["sgu_b"][l][4 * j:4 * j + 4].T)
    im["slg"] = np.broadcast_to(W["sgu_ln_g"][l][j * 512:(j + 1) * 512], (128, 512)).copy()
    im["slb"] = np.broadcast_to(W["sgu_ln_b"][l][j * 512:(j + 1) * 512], (128, 512)).copy()
    ch = lambda v: np.ascontiguousarray(v[j * 512:(j + 1) * 512].reshape(4, 128).T)
    cw = W["conv_w"][l][:, j * 512:(j + 1) * 512].reshape(4, 4, 128)
    im["cw"] = np.ascontiguousarray(cw.transpose(2, 1, 0)).reshape(128, 16)
    im["cb"] = ch(W["conv_b"][l])
    im["lwa"] = np.ascontiguousarray(W["lru_wa"][l][4 * j:4 * j + 4].transpose(1, 0, 2)).reshape(128, 512)
    im["lwx"] = np.ascontiguousarray(W["lru_wx"][l][4 * j:4 * j + 4].transpose(1, 0, 2)).reshape(128, 512)
    im["lba"] = ch(W["lru_ba"][l]); im["lbx"] = ch(W["lru_bx"][l]); im["lam"] = ch(W["lru_lambda"][l])
    im["identf"] = np.eye(128, dtype=f32)
    im["memT"] = arr_k(np.ascontiguousarray(mem_b.T), 256)[0]
    wkv = W["w_mem_kv"][l]
    im["wmk"] = arr_k(wkv[:, 2 * j * 256:2 * j * 256 + 512], 512)[0]
    im["wmv"] = arr_k(wkv[:, 1024 + 2 * j * 256:1024 + 2 * j * 256 + 512], 512)[0]
    return im


import numpy as np
WSEL = 1920; WWIN = 1152; WCMP = 3392; NEG = -30000.0

def rel_bucket_np(dist):
    n = np.maximum(dist, 0)
    nf = np.maximum(n, 1).astype(np.float32)
    large = 16 + (np.log(nf / np.float32(16)) / np.float32(np.log(64.0)) * np.float32(16)).astype(np.int32)
    return np.where(n < 16, n, np.minimum(large, 31))

def nsa_inputs(j, l, W):
    f32 = np.float32
    im = {}
    for nm, key in (("w1k", "cmp_w1_k"), ("w1v", "cmp_w1_v")):
        w1 = W[key][l].reshape(32, 64, 256)
        a = np.concatenate([w1, w1], axis=1).transpose(1, 0, 2)
        im[nm] = np.ascontiguousarray(a).reshape(128, 32 * 256)
    for nm, key in (("pek", "cmp_pe_k"), ("pev", "cmp_pe_v")):
        pe = W[key][l]
        im[nm] = np.ascontiguousarray(np.concatenate([pe.T, pe.T], axis=0))
    w2k = W["cmp_w2_k"][l].reshape(2, 128, 64)
    w2kp = np.zeros((128, 2, 2, 128), f32)
    for kvl in range(2):
        for hc in range(2):
            w2kp[:, kvl, hc, kvl * 64:(kvl + 1) * 64] = w2k[hc]
    im["w2kp"] = w2kp.reshape(128, 512)
    im["w2v"] = np.ascontiguousarray(W["cmp_w2_v"][l].reshape(2, 128, 64).transpose(1, 0, 2)).reshape(128, 128)
    gb = W["nsa_gate_b"][l]
    idx = np.concatenate([np.arange(br * 16 + 2 * j * 4, br * 16 + 2 * j * 4 + 8) for br in range(3)])
    im["gateb"] = np.broadcast_to(gb[idx], (128, 24)).copy()
    c_start = np.arange(255) * 16; s_start = np.arange(64) * 64
    ov = np.clip(np.minimum(c_start[:, None] + 32, s_start[None, :] + 64) - np.maximum(c_start[:, None], s_start[None, :]), 0, None).astype(f32) / 32
    ovp = np.zeros((256, 64), f32); ovp[:255] = ov
    im["ovl"] = np.ascontiguousarray(ovp.reshape(2, 128, 64).transpose(1, 0, 2)).reshape(128, 128)
    rb = W["rel_bias"]
    kk = np.arange(128)[:, None]
    d_sel = np.arange(WSEL)[None, :] - 384 - kk
    d_win = np.arange(WWIN)[None, :] - 384 - kk
    d_cmp = np.arange(WCMP)[None, :] - 16 * kk - 31
    b_sel = rel_bucket_np(d_sel); b_win = rel_bucket_np(d_win); b_cmp = rel_bucket_np(d_cmp)
    hsel = np.zeros((8, 128, WSEL), f32); hwin = np.zeros((8, 128, WWIN), f32); hcmp = np.zeros((8, 128, WCMP), f32)
    for hl in range(8):
        kvl, g = divmod(hl, 4)
        hq = (2 * j + kvl) * 4 + g
        col = rb[:, hq]
        hsel[hl] = np.where(d_sel >= 0, col[b_sel], f32(NEG))
        hwin[hl] = np.where((d_win >= 0) & (d_win < 256), col[b_win], f32(NEG))
        hcmp[hl] = np.where(d_cmp >= 0, col[b_cmp], f32(NEG))
    im["hsel"] = hsel; im["hwin"] = hwin; im["hcmp"] = hcmp
    hqs = [(2 * j + hl // 4) * 4 + hl % 4 for hl in range(8)]
    im["b31"] = np.broadcast_to(rb[31, hqs], (128, 8)).copy()
    pos = np.arange(4096); qb = pos // 64; jj = np.arange(64)
    forced = (jj[None, :] == 0) | (jj[None, :] == qb[:, None]) | (jj[None, :] == qb[:, None] - 1)
    future = jj[None, :] > qb[:, None]
    addc = np.where(future, f32(-1e4), np.where(forced, f32(1e4), f32(0))).astype(f32)
    im["addc"] = np.ascontiguousarray(addc.reshape(32, 128, 64).transpose(1, 0, 2)).reshape(128, 32 * 64)
    im["emat"] = (np.arange(4096)[None, :] // 64 == np.arange(64)[:, None]).astype(f32)
    im["identn"] = np.eye(128, dtype=f32); im["identnf"] = np.eye(128, dtype=f32)
    return im


from concourse.bass_utils import run_bass_kernel_spmd

DEPTH = 2
BATCH = 4
PAIRS = [[0, 1], [2, 3], [4, 5], [6, 7]]


def lay_s2_weights(w_gate, w_branch, w_out):
    wgA = w_gate.reshape(16, 128, 4, 16, 128)
    wgA = np.ascontiguousarray(wgA.transpose(3, 2, 1, 0, 4)).reshape(64, 128, 2048)
    wbA = w_branch.reshape(4, 8, 128, 16, 128)
    wbA = np.ascontiguousarray(wbA.transpose(3, 0, 2, 1, 4)).reshape(64, 128, 1024)
    woA = w_out.reshape(16, 128, 4, 512)
    woA = np.ascontiguousarray(woA.transpose(2, 1, 0, 3)).reshape(4, 128, 8192)
    return wgA, wbA, woA


def build_fused():
    nc = bass.Bass("TRN2", target_bir_lowering=False)
    p = Prog(nc)
    ar = Arena(p, "arena", 50000)
    ps = [p.psum("ps%d" % i, [128, 512])[:] for i in range(8)]
    x_in = nc.dram_tensor("x_full", [S, D], F32, kind="ExternalInput").ap()
    y_out = nc.dram_tensor("y_out", [2048, D], F32, kind="ExternalOutput").ap()
    o_loc = [nc.dram_tensor("o_loc%d" % l, [S, 2048], BF16).ap() for l in range(DEPTH)]
    o_all = [nc.dram_tensor("o_all%d" % l, [2 * S, 2048], BF16).ap() for l in range(DEPTH)]
    y0 = nc.dram_tensor("y0", [2048, D], F32).ap()
    xb_loc = nc.dram_tensor("xb_loc", [2048, D], BF16).ap()
    xg = nc.dram_tensor("xg", [S, D], BF16).ap()
    x_loc = nc.dram_tensor("x_loc", [2048, D], F32).ap()
    o_mine = [nc.dram_tensor("o_mine%d" % l, [2, 2048, 2048], BF16).ap() for l in range(DEPTH)]
    p.dma("sp", x_loc.rearrange("(a b) n -> a (b n)", a=128),
          lambda eng: x_in[bass.ds(p.par(eng) * 2048, 2048), :].rearrange("(a b) n -> a (b n)", a=128))
    for l in range(int(os.environ.get("FUSE_L", DEPTH))):
        sfx = "_%d" % l
        if os.environ.get("FUSE_NOS1"):
            nc.dram_tensor("dummy_in" + sfx, [128, 8], F32, kind="ExternalInput")
        else:
          build_s1(nc, p, ar, ps, sfx=sfx, x_src=(x_in if l == 0 else xg), o_dst=o_loc[l], x_tile_row=((lambda t: t * 128) if l == 0 else (lambda t: (((t % 16) // 4) * 2 + t // 16) * 512 + (t % 4) * 128)))
        for k in range(8):
            p.collective("AllGather", PAIRS, o_loc[l][k * 512:(k + 1) * 512, :], o_all[l][k * 1024:(k + 1) * 1024, :])
        p.barrier()
        for r in range(2):
            p.dma("sp", o_mine[l][r].rearrange("(k i) n -> k i n", k=4),
                  lambda eng, r=r, l=l: o_all[l].rearrange("(k r i) n -> k r i n", k=8, r=2)[bass.ds(p.par(eng) * 4, 4), r])
        p.barrier()

        def x_rows(r0, l=l):
            if l == 0:
                return x_loc[r0:r0 + 128, :]
            return y0[r0:r0 + 128, :]

        def o_load(lbuf, lk, r0, l=l):
            dst = lbuf.rearrange("p (i r n) -> p i r n", i=4, r=2)
            for r in range(2):
                p.dma("pool", dst[:, :, r, :], o_mine[l][r, r0:r0 + 128, :].rearrange("p (i n) -> p i n", i=4), writes=[lk])

        def y_store(r_, rk, r0, l=l):
            if l == 0:
                p.dma("sp", y0[r0:r0 + 128, :], r_, reads=[rk])
                p.dma("pool", xb_loc[r0:r0 + 128, :], r_, reads=[rk])
            else:
                p.dma("sp", y_out[r0:r0 + 128, :], r_, reads=[rk])
        if not os.environ.get("FUSE_NOS2"):
            build_s2(nc, p, ar, ps, T2=2048, sfx=sfx, x_rows=x_rows, o_load=o_load, y_store=y_store)
        if l == 0:
            for k in range(4):
                p.collective("AllGather", PAIRS, xb_loc[k * 512:(k + 1) * 512, :], xg[k * 1024:(k + 1) * 1024, :])
            p.barrier()
    p.emit()
    return nc


def kernel(**inputs):
    W = {k: np.asarray(v) for k, v in inputs.items()}
    x = W["x"].astype(np.float32, copy=False)
    mem = W["mem"]
    ident = np.eye(128, dtype=np.float32)
    nc = build_fused()
    shared = {}
    for l in range(DEPTH):
        wgA, wbA, woA = lay_s2_weights(W["w_in"][l][:, O_MG:O_MG + 8192], W["w_branch"][l], W["w_out"][l])
        s2 = {"wg": wgA, "wb": wbA, "wo": woA, "lng": np.broadcast_to(W["ln_g"][l], (128, 2048)).copy(),
              "lnb": np.broadcast_to(W["ln_b"][l], (128, 2048)).copy(), "ident": ident}
        for j in range(2):
            d = nsa_inputs(j, l, W)
            shared[(l, j)] = (d, s2)
    in_maps = []
    for c in range(8):
        b, j = divmod(c, 2)
        im = {"x_full": np.ascontiguousarray(x[b])}
        for l in range(DEPTH):
            d, s2 = shared[(l, j)]
            s1 = s1_inputs(j, None, mem[b], l, W)
            s1.pop("x_tok1")
            for k, v in list(s1.items()) + list(d.items()) + list(s2.items()):
                im[k + "_%d" % l] = v
        in_maps.append(im)
    res = run_bass_kernel_spmd(nc, in_maps, core_ids=list(range(8)))
    out = np.empty_like(x)
    for c in range(8):
        b, j = divmod(c, 2)
        out[b, j * 2048:(j + 1) * 2048] = np.asarray(res.results[c]["y_out"])
    return out
```

```python
import os
import contextlib
import numpy as np
import concourse.bass as bass
import concourse.mybir as mybir

F32 = mybir.dt.float32
BF16 = mybir.dt.bfloat16
AF = mybir.ActivationFunctionType
ALU = mybir.AluOpType
AX = mybir.AxisListType

ENGS = ("pe", "act", "dve", "pool", "sp")
SEM_LIMIT = 20000
N_DMA_SEMS = 16


class Prog:
    def __init__(self, nc):
        self.nc = nc
        self.stack = contextlib.ExitStack()
        self.streams = {e: [] for e in ENGS}
        self.cur_sem = {}
        self._emitted_wait = {e: {} for e in ENGS}
        self.last_w = {}
        self.readers = {}
        self.dma_sems = None
        self.dma_rr = 0
        self.dma_cnt = []
        self.nsem = 0
        self.all_dma_tokens = []
        self.extra_tokens = []
        self._cap = None

    def sem(self, name):
        self.nsem += 1
        return self.stack.enter_context(self.nc.semaphore(f"{name}_{self.nsem}"))

    def sbuf(self, name, shape, dt):
        return self.stack.enter_context(self.nc.sbuf_tensor(name, list(shape), dt))

    def psum(self, name, shape, dt=F32):
        return self.stack.enter_context(self.nc.psum_tensor(name, list(shape), dt))

    def _eng_token(self, e):
        if e not in self.cur_sem or self.cur_sem[e][1] >= SEM_LIMIT:
            self.cur_sem[e] = [self.sem("e" + e), 0]
        cs = self.cur_sem[e]
        cs[1] += 1
        return (cs[0], cs[1])

    def _deps(self, e, reads, writes, same_engine_ok):
        deps = []
        for r in reads:
            t = self.last_w.get(r)
            if t is not None:
                deps.append(t)
        for w in writes:
            t = self.last_w.get(w)
            if t is not None:
                deps.append(t)
            deps.extend(self.readers.get(w, ()))
        need = {}
        for (s, v, src) in deps:
            if same_engine_ok and src == e:
                continue
            k = id(s)
            if k not in need or need[k][1] < v:
                need[k] = (s, v)
        out = []
        ew = self._emitted_wait[e]
        for k, (s, v) in need.items():
            if ew.get(k, 0) >= v:
                continue
            out.append((s, v))
            ew[k] = v
        return out

    def _record(self, tok, reads, writes):
        for r in reads:
            self.readers.setdefault(r, []).append(tok)
        for w in writes:
            self.last_w[w] = tok
            self.readers[w] = []

    def op(self, e, fn, reads=(), writes=(), same_ok=None):
        if self._cap is not None:
            self._cap.append(("op", (e, fn, tuple(reads), tuple(writes), same_ok), {}))
            return
        if same_ok is None:
            same_ok = (e == "pe")
        waits = self._deps(e, reads, writes, same_ok)
        s, v = self._eng_token(e)
        self.streams[e].append((waits, fn, (s, 1)))
        self._record((s, v, e), reads, writes)

    def dma(self, q, out, in_, reads=(), writes=(), **kw):
        if self._cap is not None:
            self._cap.append(("dma", (q, out, in_, tuple(reads), tuple(writes)), dict(kw)))
            return
        if self.dma_sems is None:
            self.dma_sems = [self.sem("dma") for _ in range(N_DMA_SEMS)]
            self.dma_cnt = [0] * N_DMA_SEMS
        i = self.dma_rr
        self.dma_rr = (self.dma_rr + 1) % N_DMA_SEMS
        s = self.dma_sems[i]
        waits = self._deps(q, reads, writes, False)
        c = self.dma_cnt[i]
        if c > 0 and self._emitted_wait[q].get(id(s), 0) < 16 * c:
            waits.append((s, 16 * c))
            self._emitted_wait[q][id(s)] = 16 * c
        self.dma_cnt[i] = c + 1
        tok = (s, 16 * (c + 1), "dma")
        def fn(eng, out=out, in_=in_, kw=kw):
            o = out(eng) if callable(out) else out
            i = in_(eng) if callable(in_) else in_
            return eng.dma_start(out=o, in_=i, **kw)
        self.streams[q].append((waits, fn, (s, 16)))
        self._record(tok, reads, writes)
        self.all_dma_tokens.append(tok)


    def capture(self):
        self._cap = []

    def end_capture(self):
        c = self._cap
        self._cap = None
        return c

    def replay(self, lists, width):
        active = []
        i = 0
        while i < len(lists) or active:
            while len(active) < width and i < len(lists):
                if lists[i]:
                    active.append([lists[i], 0])
                i += 1
            for a in list(active):
                kind, args, kw = a[0][a[1]]
                a[1] += 1
                if kind == "op":
                    self.op(*args)
                else:
                    self.dma(*args, **kw)
                if a[1] >= len(a[0]):
                    active.remove(a)

    def par(self, eng):
        k = id(eng)
        if not hasattr(self, "_par"):
            self._par = {}
        if k not in self._par:
            self._par[k] = eng.partition_id() % 2
        return self._par[k]

    def collective(self, kind, replica_groups, in_ap, out_ap, reads=()):
        s = self.sem("cc")
        fn = lambda eng: eng.collective_compute(kind, mybir.AluOpType.bypass, replica_groups=replica_groups, ins=[in_ap], outs=[out_ap])
        waits = self._deps("pool", list(reads), [], False)
        self.streams["pool"].append((waits, fn, (s, 1)))
        self.extra_tokens.append((s, 1))

    def barrier(self):
        toks = []
        for e, cs in self.cur_sem.items():
            toks.append((cs[0], cs[1]))
        if self.dma_sems is not None:
            for s, c in zip(self.dma_sems, self.dma_cnt):
                if c > 0:
                    toks.append((s, 16 * c))
        toks.extend(self.extra_tokens)
        for e in ENGS:
            waits = []
            ew = self._emitted_wait[e]
            for (s, v) in toks:
                if ew.get(id(s), 0) >= v:
                    continue
                waits.append((s, v))
                ew[id(s)] = v
            if waits:
                self.streams[e].append((waits, None, None))
        self.last_w = {}
        self.readers = {}

    def finish(self):
        waits = []
        if self.dma_sems is not None:
            for s, c in zip(self.dma_sems, self.dma_cnt):
                if c > 0:
                    waits.append((s, 16 * c))
        self.streams["sp"].append((waits, None, None))

    def emit(self):
        nc = self.nc
        self.finish()
        with nc.Block() as block:
            def run(eng, stream):
                for waits, fn, inc in stream:
                    for (s, v) in waits:
                        eng.wait_ge(s, v)
                    if fn is not None:
                        ins = fn(eng)
                        ins.then_inc(inc[0], inc[1])

            @block.tensor
            def _(eng):
                run(eng, self.streams["pe"])

            @block.scalar
            def _(eng):
                run(eng, self.streams["act"])

            @block.vector
            def _(eng):
                run(eng, self.streams["dve"])

            @block.gpsimd
            def _(eng):
                run(eng, self.streams["pool"])

            @block.sync
            def _(eng):
                run(eng, self.streams["sp"])
        self.stack.close()


D = 2048
ALPHA = 4 ** 0.25
LN_EPS = 1e-5


class Arena:
    def __init__(self, p, name, words):
        self.t = p.sbuf(name, [128, words], F32)
        self.words = words
        self.off = 0

    def reset(self):
        self.off = 0

    def alloc(self, shape, dt):
        n = 1
        for s in shape:
            n *= s
        nbytes = n * (4 if dt == F32 else 2)
        nw = (nbytes + 3) // 4
        nw = (nw + 7) // 8 * 8
        assert self.off + nw <= self.words, (self.off, nw, self.words)
        ap = self.t[:, self.off:self.off + nw]
        self.off += nw
        if dt != F32:
            ap = ap.bitcast(dt)
        ap = ap[:, 0:n]
        if len(shape) > 1:
            names = " ".join(f"a{i}" for i in range(len(shape)))
            kw = {f"a{i}": shape[i] for i in range(len(shape))}
            ap = ap.rearrange(f"p ({names}) -> p {names}", **kw)
        return ap


def build_s2(nc, p, ar, ps, T2=2048, TP=1024, sfx="", x_rows=None, o_load=None, y_store=None, after_pass=None):
    nI = lambda n, s, d=F32: nc.dram_tensor(n + sfx, s, d, kind="ExternalInput").ap()
    if x_rows is None:
        x_tok = nI("x_tok", [T2, D])
        o_tok = nI("o_tok", [T2, 2 * D], BF16)
        x_rows = lambda r0: x_tok[r0:r0 + 128, :]
        o_load = lambda lbuf, lk, r0: p.dma("pool", lbuf, o_tok[r0:r0 + 128, :], writes=[lk])
    wg = nI("wg", [64, 128, 16 * 128])
    wb = nI("wb", [64, 128, 8 * 128])
    wo = nI("wo", [4, 128, 16 * 512])
    lng = nI("lng", [128, D])
    lnb = nI("lnb", [128, D])
    ident_d = nI("ident", [128, 128])
    if y_store is None:
        y_out = nc.dram_tensor("y_out" + sfx, [T2, D], F32, kind="ExternalOutput").ap()
        y_store = lambda r_, rk, r0: p.dma("sp", y_out[r0:r0 + 128, :], r_, reads=[rk])

    ar.reset()
    ident = ar.alloc([128], BF16)
    p.dma("pool", ident, ident_d, writes=["ident"])
    lg = ar.alloc([D], F32); lb = ar.alloc([D], F32)
    p.dma("sp", lg, lng, writes=["lg"]); p.dma("sp", lb, lnb, writes=["lb"])
    epsT = ar.alloc([1], F32)
    p.op("pool", lambda e: e.memset(epsT, LN_EPS), writes=["eps"])
    NT = TP // 128
    xT = ar.alloc([16, TP], BF16)
    mT = ar.alloc([16, TP], BF16)
    big = ar.alloc([32 * TP], BF16)
    oT = big.rearrange("p (f t) -> p f t", f=32)
    woS = big.rearrange("p (k n) -> p k n", k=16)
    assert 32 * TP == 16 * 2048
    mark = ar.off
    ldb = [ar.alloc([2 * D], BF16) for _ in range(2)]
    ar.off = mark
    wgS = [ar.alloc([16, 128], BF16) for _ in range(2)]
    wbS = [ar.alloc([8, 128], BF16) for _ in range(2)]
    sg = [ar.alloc([512], F32) for _ in range(2)]
    tmp = [ar.alloc([512], F32) for _ in range(2)]
    macc = [ar.alloc([512], F32) for _ in range(TP // 512)]
    ar.off = mark
    x32 = [ar.alloc([D], F32) for _ in range(2)]
    rb = [ar.alloc([D], F32) for _ in range(2)]
    st = [ar.alloc([4, 6], F32) for _ in range(2)]
    mv = [ar.alloc([2], F32) for _ in range(2)]
    rstd = [ar.alloc([1], F32) for _ in range(2)]
    tpb = [ps[4].bitcast(BF16), ps[5].bitcast(BF16)]

    cnt = {"ev": 0}

    def evac_copy(out, in_, reads, writes):
        cnt["ev"] += 1
        if cnt["ev"] % 2:
            p.op("act", lambda e: e.copy(out=out, in_=in_), reads=reads, writes=writes)
        else:
            p.op("dve", lambda e: e.tensor_copy(out=out, in_=in_), reads=reads, writes=writes)

    for ps_i in range(T2 // TP):
        tok0 = ps_i * TP
        for t in range(NT):
            r0 = tok0 + t * 128
            for which, nf, dstT, key in (("x", 16, xT, "xT"), ("o", 32, oT, "big")):
                lbuf = ldb[(2 * t + (which == "o")) % 2]
                lk = "ldb%d" % ((2 * t + (which == "o")) % 2)
                if which == "x":
                    p.dma("pool", lbuf[:, 0:nf * 128], x_rows(r0), writes=[lk])
                else:
                    o_load(lbuf, lk, r0)
                for g in range(nf // 4):
                    tp = tpb[g % 2]; tk = "PS%d" % (4 + g % 2)

                    def tr(e, tp=tp, lbuf=lbuf, g=g):
                        for q in range(4):
                            f = g * 4 + q
                            ins = e.transpose(out=tp[:, q * 128:(q + 1) * 128], in_=lbuf[:, f * 128:(f + 1) * 128], identity=ident)
                        return ins
                    p.op("pe", tr, reads=[lk, "ident"], writes=[tk])
                    evac_copy(dstT[:, g * 4:(g + 1) * 4, t * 128:(t + 1) * 128],
                              tp[:, 0:512].rearrange("p (q n) -> p q n", q=4), [tk], [key])
        p.barrier()
        import os
        S2STOP = int(os.environ.get('S2STOP', '3'))
        it = 0
        for c in range(16 if S2STOP >= 2 else 0):
            for i in range(4):
                sl = c * 4 + i
                wgs = wgS[sl % 2]; wbs = wbS[sl % 2]; kg = "wg%d" % (sl % 2); kb = "wb%d" % (sl % 2)
                p.dma("pool", wgs, wg[c * 4 + i].rearrange("p (k n) -> p k n", k=16), writes=[kg])
                p.dma("pool", wbs, wb[c * 4 + i].rearrange("p (k n) -> p k n", k=8), writes=[kb])
                for tt in range(TP // 512):
                    G = ps[it % 2]; B = ps[2 + it % 2]; gk = "PS%d" % (it % 2); bk = "PS%d" % (2 + it % 2)
                    tsl = slice(tt * 512, (tt + 1) * 512)

                    def mmG(e, G=G, wgs=wgs, tsl=tsl):
                        for kc in range(16):
                            ins = e.matmul(G, lhsT=wgs[:, kc, :], rhs=xT[:, kc, tsl], start=(kc == 0), stop=(kc == 15))
                        return ins

                    def mmB(e, B=B, wbs=wbs, tsl=tsl, i=i):
                        for kc in range(8):
                            ins = e.matmul(B, lhsT=wbs[:, kc, :], rhs=oT[:, i * 8 + kc, tsl], start=(kc == 0), stop=(kc == 7))
                        return ins
                    p.op("pe", mmG, reads=[kg, "xT"], writes=[gk])
                    p.op("pe", mmB, reads=[kb, "big"], writes=[bk])
                    sgb = sg[it % 2]; sk = "sg%d" % (it % 2)
                    p.op("act", lambda e, sgb=sgb, G=G: e.activation(out=sgb, in_=G, func=AF.Sigmoid), reads=[gk], writes=[sk])
                    mk = "macc%d" % tt
                    if i == 0:
                        p.op("dve", lambda e, sgb=sgb, B=B, m=macc[tt]: e.tensor_tensor(out=m, in0=sgb, in1=B, op=ALU.mult),
                             reads=[sk, bk], writes=[mk])
                    else:
                        tb = tmp[it % 2]; tk2 = "tmp%d" % (it % 2)
                        p.op("dve", lambda e, sgb=sgb, B=B, tb=tb: e.tensor_tensor(out=tb, in0=sgb, in1=B, op=ALU.mult),
                             reads=[sk, bk], writes=[tk2])
                        if i < 3:
                            p.op("dve", lambda e, tb=tb, m=macc[tt]: e.tensor_tensor(out=m, in0=m, in1=tb, op=ALU.add),
                                 reads=[tk2, mk], writes=[mk])
                        else:
                            p.op("dve", lambda e, tb=tb, m=macc[tt], c=c, tsl=tsl: e.tensor_tensor(out=mT[:, c, tsl], in0=m, in1=tb, op=ALU.add),
                                 reads=[tk2, mk], writes=["mT"])
                    it += 1
        p.barrier()
        for nt in range(4):
            p.dma("pool", woS[:, :, nt * 512:(nt + 1) * 512], wo[nt].rearrange("p (k n) -> p k n", k=16), writes=["big"])
        y_lists = []
        for t in range(NT if S2STOP >= 3 else 0):
            r0 = tok0 + t * 128
            p.capture()
            xb_ = x32[t % 2]; xk = "x32%d" % (t % 2)
            r_ = rb[t % 2]; rk = "rb%d" % (t % 2)
            p.dma("sp", xb_, x_rows(r0), writes=[xk])
            for nt in range(4):
                Y = ps[4 + (t % 2) * 2 + nt % 2]; yk = "PS%d" % (4 + (t % 2) * 2 + nt % 2)

                def mmY(e, Y=Y, t=t, nt=nt):
                    for kc in range(16):
                        ins = e.matmul(Y, lhsT=mT[:, kc, t * 128:(t + 1) * 128], rhs=woS[:, kc, nt * 512:(nt + 1) * 512],
                                       start=(kc == 0), stop=(kc == 15))
                    return ins
                p.op("pe", mmY, reads=["mT", "big"], writes=[yk])
                p.op("dve", lambda e, Y=Y, xb_=xb_, r_=r_, nt=nt: e.scalar_tensor_tensor(
                    out=r_[:, nt * 512:(nt + 1) * 512], in0=xb_[:, nt * 512:(nt + 1) * 512], scalar=ALPHA, in1=Y,
                    op0=ALU.mult, op1=ALU.add), reads=[yk, xk], writes=[rk])
            s_ = st[t % 2]; sk = "st%d" % (t % 2); m_ = mv[t % 2]; mk2 = "mv%d" % (t % 2); rs = rstd[t % 2]; rsk = "rstd%d" % (t % 2)

            def stats(e, s_=s_, r_=r_):
                for q in range(4):
                    ins = e.bn_stats(out=s_[:, q, :], in_=r_[:, q * 512:(q + 1) * 512])
                return ins
            p.op("dve", stats, reads=[rk], writes=[sk])
            p.op("dve", lambda e, s_=s_, m_=m_: e.bn_aggr(out=m_, in_=s_.rearrange("p a b -> p (a b)")), reads=[sk], writes=[mk2])
            p.op("act", lambda e, m_=m_, rs=rs: e.activation(out=rs, in_=m_[:, 1:2], func=AF.Sqrt, bias=epsT, scale=1.0),
                 reads=[mk2, "eps"], writes=[rsk])
            p.op("dve", lambda e, rs=rs: e.reciprocal(out=rs, in_=rs), reads=[rsk], writes=[rsk])
            p.op("dve", lambda e, r_=r_, m_=m_, rs=rs: e.tensor_scalar(out=r_, in0=r_, scalar1=m_[:, 0:1], scalar2=rs,
                                                                    op0=ALU.subtract, op1=ALU.mult), reads=[rk, mk2, rsk], writes=[rk])
            p.op("pool", lambda e, r_=r_: e.tensor_tensor(out=r_, in0=r_, in1=lg, op=ALU.mult), reads=[rk, "lg"], writes=[rk])
            p.op("pool", lambda e, r_=r_: e.tensor_tensor(out=r_, in0=r_, in1=lb, op=ALU.add), reads=[rk, "lb"], writes=[rk])
            y_store(r_, rk, r0)
            y_lists.append(p.end_capture())
        p.replay(y_lists, 2)
        p.barrier()
        if after_pass is not None:
            after_pass(ps_i)


S = 4096
D = 2048
NFM = 20
NTM = 7
GC = 1.5957691216057308


def build_s1(nc, p, ar, ps, sfx="", phases=("A", "G", "L", "M", "N"), x_src=None, o_dst=None, x_tile_row=None, after_q=None):
    nI = lambda n, s, d=F32: nc.dram_tensor(n + sfx, s, d, kind="ExternalInput").ap()
    x_tok = nI("x_tok1", [S, D]) if x_src is None else x_src
    wfm = nI("wfm", [NFM, 128, 16 * 128])
    wtm = nI("wtm", [NTM, 128, 16 * 512])
    ident_d = nI("ident1", [128, 128])
    o_out = nc.dram_tensor("o_out" + sfx, [S, 2048], BF16, kind="ExternalOutput").ap() if o_dst is None else o_dst
    fmb = nc.dram_tensor("fmb" + sfx, [12, 128, S], BF16).ap()
    fmf = nc.dram_tensor("fmf" + sfx, [8, 128, S], F32).ap()
    tm = nc.dram_tensor("tm" + sfx, [S, NTM * 512], F32).ap()
    cnt = {"ev": 0}

    def evac(out, in_, reads, writes, scale=None):
        cnt["ev"] += 1
        if cnt["ev"] % 2:
            if scale is None:
                p.op("act", lambda e: e.copy(out=out, in_=in_), reads=reads, writes=writes)
            else:
                p.op("act", lambda e: e.mul(out=out, in_=in_, mul=scale) if False else e.activation(out=out, in_=in_, func=AF.Copy, scale=scale), reads=reads, writes=writes)
        else:
            if scale is None:
                p.op("dve", lambda e: e.tensor_copy(out=out, in_=in_), reads=reads, writes=writes)
            else:
                p.op("dve", lambda e: e.tensor_scalar(out=out, in0=in_, scalar1=scale, scalar2=None, op0=ALU.mult), reads=reads, writes=writes)

    if "A" in phases:
        ar.reset()
        ident = ar.alloc([128], BF16)
        p.dma("pool", ident, ident_d, writes=["ident"])
        xT = ar.alloc([16, S], BF16)
        ldb = [ar.alloc([D], BF16) for _ in range(2)]
        tpb = [ps[4].bitcast(BF16), ps[5].bitcast(BF16)]
        for t in range(S // 128):
            lbuf = ldb[t % 2]; lk = "ldb%d" % (t % 2)
            xr0 = t * 128 if x_tile_row is None else x_tile_row(t)
            p.dma("pool", lbuf, x_tok[xr0:xr0 + 128, :], writes=[lk])
            for g in range(4):
                tp = tpb[g % 2]; tk = "PS%d" % (4 + g % 2)

                def tr(e, tp=tp, lbuf=lbuf, g=g):
                    for q in range(4):
                        f = g * 4 + q
                        ins = e.transpose(out=tp[:, q * 128:(q + 1) * 128], in_=lbuf[:, f * 128:(f + 1) * 128], identity=ident)
                    return ins
                p.op("pe", tr, reads=[lk, "ident"], writes=[tk])
                evac(xT[:, g * 4:(g + 1) * 4, t * 128:(t + 1) * 128], tp[:, 0:512].rearrange("p (q n) -> p q n", q=4), [tk], ["xT"])
        wS = [ar.alloc([16, 128], BF16) for _ in range(2)]
        stb = [ar.alloc([512], BF16) for _ in range(4)]
        stf = [ar.alloc([512], F32) for _ in range(4)]
        it = 0
        for ft in range(NFM):
            w_ = wS[ft % 2]; wk = "wS%d" % (ft % 2)
            p.dma("pool", w_, wfm[ft].rearrange("p (k n) -> p k n", k=16), writes=[wk])
            for Tq in range(S // 512):
                P_ = ps[it % 4]; pk = "PS%d" % (it % 4)

                def mm(e, P_=P_, w_=w_, Tq=Tq):
                    for kc in range(16):
                        ins = e.matmul(P_, lhsT=w_[:, kc, :], rhs=xT[:, kc, Tq * 512:(Tq + 1) * 512], start=(kc == 0), stop=(kc == 15))
                    return ins
                p.op("pe", mm, reads=[wk, "xT"], writes=[pk])
                if ft < 12:
                    sb = stb[it % 4]; sk = "stb%d" % (it % 4)
                    sc = 0.125 if ft < 4 else (1.0 / 16 if 8 <= ft < 12 else None)
                    evac(sb, P_, [pk], [sk], scale=sc)
                    p.dma("sp", fmb[ft, :, Tq * 512:(Tq + 1) * 512], sb, reads=[sk])
                else:
                    sb = stf[it % 4]; sk = "stf%d" % (it % 4)
                    evac(sb, P_, [pk], [sk])
                    p.dma("sp", fmf[ft - 12, :, Tq * 512:(Tq + 1) * 512], sb, reads=[sk])
                it += 1
        wT_ = [ar.alloc([16, 512], BF16) for _ in range(2)]
        for tg in range(NTM):
            w_ = wT_[tg % 2]; wk = "wT%d" % (tg % 2)
            p.dma("pool", w_, wtm[tg].rearrange("p (k n) -> p k n", k=16), writes=[wk])
            ncol = 512 if tg < 6 else 280
            for t in range(S // 128):
                P_ = ps[it % 4]; pk = "PS%d" % (it % 4)

                def mm(e, P_=P_, w_=w_, t=t, ncol=ncol):
                    for kc in range(16):
                        ins = e.matmul(P_[:, 0:ncol], lhsT=xT[:, kc, t * 128:(t + 1) * 128], rhs=w_[:, kc, 0:ncol], start=(kc == 0), stop=(kc == 15))
                    return ins
                p.op("pe", mm, reads=[wk, "xT"], writes=[pk])
                sb = stf[it % 4]; sk = "stf%d" % (it % 4)
                evac(sb[:, 0:ncol], P_[:, 0:ncol], [pk], [sk])
                p.dma("sp", tm[t * 128:(t + 1) * 128, tg * 512:tg * 512 + ncol], sb[:, 0:ncol], reads=[sk])
                it += 1
        p.barrier()

    if "G" in phases:
        sgw_d = nI("sgw", [128, 4 * 128])
        sgb_d = nI("sgb", [128, 4])
        slg_d = nI("slg", [128, 512]); slb_d = nI("slb", [128, 512])
        ar.reset()
        sgw = ar.alloc([4, 128], BF16); sgb = ar.alloc([4], F32)
        slg = ar.alloc([512], F32); slb = ar.alloc([512], F32)
        p.dma("pool", sgw, sgw_d.rearrange("p (g t) -> p g t", g=4), writes=["sgw"])
        p.dma("sp", sgb, sgb_d, writes=["sgb"]); p.dma("sp", slg, slg_d, writes=["slg"]); p.dma("sp", slb, slb_d, writes=["slb"])
        epsT = ar.alloc([1], F32)
        p.op("pool", lambda e: e.memset(epsT, 1e-5), writes=["eps"])
        NB = 4
        g_lists = []
        inb = [ar.alloc([2048], F32) for _ in range(NB)]
        t1 = [ar.alloc([1536], F32) for _ in range(NB)]
        gl_ = [ar.alloc([1536], F32) for _ in range(NB)]
        vn = [ar.alloc([512], BF16) for _ in range(NB)]
        st = [ar.alloc([2, 6], F32) for _ in range(NB)]
        mv = [ar.alloc([2], F32) for _ in range(NB)]
        rs = [ar.alloc([1], F32) for _ in range(NB)]
        sl = [ar.alloc([512], F32) for _ in range(NB)]
        oa = [ar.alloc([512], F32) for _ in range(NB)]
        ob = [ar.alloc([512], BF16) for _ in range(NB)]
        for c in range(S // 128):
            b = c % NB
            p.capture()
            K = lambda n: "%s%d" % (n, b)
            X = inb[b]; T1 = t1[b]; GL = gl_[b]
            p.dma("sp", X, tm[c * 128:(c + 1) * 128, 0:2048], writes=[K("inb")])
            uv = X[:, 0:1536]
            p.op("dve", lambda e, T1=T1, uv=uv: e.tensor_tensor(out=T1, in0=uv, in1=uv, op=ALU.mult), reads=[K("inb")], writes=[K("t1")])
            p.op("dve", lambda e, T1=T1: e.tensor_scalar(out=T1, in0=T1, scalar1=0.044715, scalar2=1.0, op0=ALU.mult, op1=ALU.add), reads=[K("t1")], writes=[K("t1")])
            p.op("pool", lambda e, T1=T1, uv=uv: e.tensor_tensor(out=T1, in0=T1, in1=uv, op=ALU.mult), reads=[K("t1"), K("inb")], writes=[K("t1")])
            p.op("act", lambda e, T1=T1: e.activation(out=T1, in_=T1, func=AF.Sigmoid, scale=GC), reads=[K("t1")], writes=[K("t1")])
            p.op("pool", lambda e, T1=T1, uv=uv, GL=GL: e.tensor_tensor(out=GL, in0=T1, in1=uv, op=ALU.mult), reads=[K("t1"), K("inb")], writes=[K("gl")])
            S_ = st[b]; M_ = mv[b]; R_ = rs[b]

            def stats(e, S_=S_, GL=GL):
                for q in range(2):
                    ins = e.bn_stats(out=S_[:, q, :], in_=GL[:, 512 + q * 512:512 + (q + 1) * 512])
                return ins
            p.op("dve", stats, reads=[K("gl")], writes=[K("st")])
            p.op("dve", lambda e, S_=S_, M_=M_: e.bn_aggr(out=M_, in_=S_.rearrange("p a b -> p (a b)")), reads=[K("st")], writes=[K("mv")])
            p.op("act", lambda e, M_=M_, R_=R_: e.activation(out=R_, in_=M_[:, 1:2], func=AF.Sqrt, bias=epsT, scale=1.0), reads=[K("mv"), "eps"], writes=[K("rs")])
            p.op("dve", lambda e, R_=R_: e.reciprocal(out=R_, in_=R_), reads=[K("rs")], writes=[K("rs")])
            VH = None
            return_vh = lambda: None
            vm = GL[:, 512:1024]
            p.op("dve", lambda e, vm=vm, M_=M_, R_=R_: e.tensor_scalar(out=vm, in0=vm, scalar1=M_[:, 0:1], scalar2=R_, op0=ALU.subtract, op1=ALU.mult),
                 reads=[K("gl"), K("mv"), K("rs")], writes=[K("gl")])
            p.op("pool", lambda e, vm=vm: e.tensor_tensor(out=vm, in0=vm, in1=slg, op=ALU.mult), reads=[K("gl"), "slg"], writes=[K("gl")])
            VN = vn[b]
            p.op("pool", lambda e, vm=vm, VN=VN: e.tensor_tensor(out=VN, in0=vm, in1=slb, op=ALU.add), reads=[K("gl"), "slb"], writes=[K("vn")])
            P_ = ps[c % 4]; pk = "PS%d" % (c % 4)

            def mm(e, P_=P_, VN=VN):
                for g in range(4):
                    ins = e.matmul(P_[:, g * 128:(g + 1) * 128], lhsT=sgw[:, g, :], rhs=VN[:, g * 128:(g + 1) * 128], start=True, stop=True)
                return ins
            p.op("pe", mm, reads=["sgw", K("vn")], writes=[pk])
            SL = sl[b]; OA = oa[b]; OB = ob[b]
            p.op("act", lambda e, SL=SL, X=X: e.activation(out=SL, in_=X[:, 1536:2048], func=AF.Silu), reads=[K("inb")], writes=[K("sl")])

            def fin(e, P_=P_, OA=OA, GL=GL):
                for g in range(4):
                    ins = e.scalar_tensor_tensor(out=OA[:, g * 128:(g + 1) * 128], in0=P_[:, g * 128:(g + 1) * 128], scalar=sgb[:, g:g + 1],
                                                 in1=GL[:, g * 128:(g + 1) * 128], op0=ALU.add, op1=ALU.mult)
                return ins
            p.op("dve", fin, reads=[pk, "sgb", K("gl")], writes=[K("oa")])
            p.op("pool", lambda e, OA=OA, SL=SL, OB=OB: e.tensor_tensor(out=OB, in0=OA, in1=SL, op=ALU.mult), reads=[K("oa"), K("sl")], writes=[K("ob")])
            p.dma("sp", o_out[c * 128:(c + 1) * 128, 0:512], OB, reads=[K("ob")])
            g_lists.append(p.end_capture())
        p.replay(g_lists, 4)
        p.barrier()

    if "L" in phases:
        cw_d = nI("cw", [128, 16]); cb_d = nI("cb", [128, 4])
        lwa_d = nI("lwa", [128, 4 * 128]); lwx_d = nI("lwx", [128, 4 * 128])
        lba_d = nI("lba", [128, 4]); lbx_d = nI("lbx", [128, 4]); lam_d = nI("lam", [128, 4])
        identf_d = nI("identf", [128, 128])
        ar.reset()
        cw = ar.alloc([4, 4], F32); cb = ar.alloc([4], F32)
        lwa = ar.alloc([4, 128], BF16); lwx = ar.alloc([4, 128], BF16)
        lba = ar.alloc([4], F32); lbx = ar.alloc([4], F32); lam = ar.alloc([4], F32); c8 = ar.alloc([4], F32)
        identf = ar.alloc([128], F32)
        p.dma("sp", cw, cw_d.rearrange("p (c k) -> p c k", c=4), writes=["cw"]); p.dma("sp", cb, cb_d, writes=["cb"])
        p.dma("pool", lwa, lwa_d.rearrange("p (c n) -> p c n", c=4), writes=["lwa"])
        p.dma("pool", lwx, lwx_d.rearrange("p (c n) -> p c n", c=4), writes=["lwx"])
        p.dma("sp", lba, lba_d, writes=["lba"]); p.dma("sp", lbx, lbx_d, writes=["lbx"]); p.dma("sp", lam, lam_d, writes=["lam"])
        p.dma("sp", identf, identf_d, writes=["identf"])
        one = ar.alloc([1], F32)
        p.op("pool", lambda e: e.memset(one, 1.0), writes=["one"])
        p.op("act", lambda e: e.activation(out=c8, in_=lam, func=AF.Exp, scale=-1.0), reads=["lam"], writes=["c8"])
        p.op("act", lambda e: e.activation(out=c8, in_=c8, func=AF.Ln, bias=one, scale=1.0), reads=["c8", "one"], writes=["c8"])
        p.op("dve", lambda e: e.tensor_scalar(out=c8, in0=c8, scalar1=-8.0, scalar2=None, op0=ALU.mult), reads=["c8"], writes=["c8"])
        carry = ar.alloc([4], F32)
        p.op("pool", lambda e: e.memset(carry, 0.0), writes=["carry0", "carry1", "carry2", "carry3"])
        NB = 4
        l_lists = []
        xin = [ar.alloc([515], F32) for _ in range(NB)]
        gin = [ar.alloc([512], F32) for _ in range(NB)]
        xc = [ar.alloc([512], F32) for _ in range(NB)]
        xcb = [ar.alloc([512], BF16) for _ in range(NB)]
        rr = [ar.alloc([512], F32) for _ in range(NB)]
        ii = [ar.alloc([512], F32) for _ in range(NB)]
        aa = [ar.alloc([512], F32) for _ in range(NB)]
        ss = [ar.alloc([512], F32) for _ in range(NB)]
        hh = [ar.alloc([512], F32) for _ in range(NB)]
        otile = [ar.alloc([4, 512], BF16) for _ in range(2)]
        it = 0
        for Tq in range(S // 512):
            OT = otile[Tq % 2]; otk = "otile%d" % (Tq % 2)
            for ct in range(4):
                b = it % NB
                p.capture()
                K = lambda n: "%s%d" % (n, b)
                XI = xin[b]; GI = gin[b]; XC = xc[b]; XB = xcb[b]; RR = rr[b]; II = ii[b]; AA = aa[b]; SS = ss[b]; HH = hh[b]
                if Tq == 0:
                    p.op("pool", lambda e, XI=XI: e.memset(XI[:, 0:3], 0.0), writes=[K("xin")])
                    p.dma("sp", XI[:, 3:515], fmf[ct, :, 0:512], writes=[K("xin")])
                else:
                    p.dma("sp", XI, fmf[ct, :, Tq * 512 - 3:(Tq + 1) * 512], writes=[K("xin")])
                p.dma("sp", GI, fmf[4 + ct, :, Tq * 512:(Tq + 1) * 512], writes=[K("gin")])
                p.op("dve", lambda e, XC=XC, XI=XI, ct=ct: e.tensor_scalar(out=XC, in0=XI[:, 3:515], scalar1=cw[:, ct, 3:4], scalar2=cb[:, ct:ct + 1],
                                                                         op0=ALU.mult, op1=ALU.add), reads=[K("xin"), "cw", "cb"], writes=[K("xc")])
                for k in range(3):
                    p.op("dve", lambda e, XC=XC, XI=XI, ct=ct, k=k: e.scalar_tensor_tensor(out=XC, in0=XI[:, k:k + 512], scalar=cw[:, ct, k:k + 1], in1=XC,
                                                                                           op0=ALU.mult, op1=ALU.add), reads=[K("xin"), K("xc"), "cw"], writes=[K("xc")])
                p.op("pool", lambda e, XC=XC, XB=XB: e.tensor_copy(out=XB, in_=XC), reads=[K("xc")], writes=[K("xcb")])
                PA = ps[it % 4]; PX = ps[4 + it % 4]; pak = "PS%d" % (it % 4); pxk = "PS%d" % (4 + it % 4)
                p.op("pe", lambda e, PA=PA, XB=XB, ct=ct: e.matmul(PA, lhsT=lwa[:, ct, :], rhs=XB, start=True, stop=True), reads=["lwa", K("xcb")], writes=[pak])
                p.op("pe", lambda e, PX=PX, XB=XB, ct=ct: e.matmul(PX, lhsT=lwx[:, ct, :], rhs=XB, start=True, stop=True), reads=["lwx", K("xcb")], writes=[pxk])
                p.op("act", lambda e, RR=RR, PA=PA, ct=ct: e.activation(out=RR, in_=PA, func=AF.Sigmoid, bias=lba[:, ct:ct + 1], scale=1.0), reads=[pak, "lba"], writes=[K("rr")])
                p.op("act", lambda e, II=II, PX=PX, ct=ct: e.activation(out=II, in_=PX, func=AF.Sigmoid, bias=lbx[:, ct:ct + 1], scale=1.0), reads=[pxk, "lbx"], writes=[K("ii")])
                p.op("act", lambda e, AA=AA, RR=RR, ct=ct: e.activation(out=AA, in_=RR, func=AF.Exp, scale=c8[:, ct:ct + 1]), reads=[K("rr"), "c8"], writes=[K("aa")])
                p.op("pool", lambda e, SS=SS, AA=AA: e.tensor_tensor(out=SS, in0=AA, in1=AA, op=ALU.mult), reads=[K("aa")], writes=[K("ss")])
                p.op("act", lambda e, SS=SS: e.activation(out=SS, in_=SS, func=AF.Sqrt, bias=one, scale=-1.0), reads=[K("ss"), "one"], writes=[K("ss")])
                p.op("pool", lambda e, II=II, XC=XC: e.tensor_tensor(out=II, in0=II, in1=XC, op=ALU.mult), reads=[K("ii"), K("xc")], writes=[K("ii")])
                p.op("pool", lambda e, II=II, SS=SS: e.tensor_tensor(out=II, in0=II, in1=SS, op=ALU.mult), reads=[K("ii"), K("ss")], writes=[K("ii")])
                p.op("dve", lambda e, HH=HH, AA=AA, II=II, ct=ct: e.tensor_tensor_scan(out=HH, data0=AA, data1=II, initial=carry[:, ct:ct + 1], op0=ALU.mult, op1=ALU.add),
                     reads=[K("aa"), K("ii"), "carry%d" % ct], writes=[K("hh")])
                p.op("dve", lambda e, HH=HH, ct=ct: e.tensor_copy(out=carry[:, ct:ct + 1], in_=HH[:, 511:512]), reads=[K("hh")], writes=["carry%d" % ct])
                p.op("act", lambda e, GI=GI: e.activation(out=GI, in_=GI, func=AF.Silu), reads=[K("gin")], writes=[K("gin")])
                p.op("pool", lambda e, HH=HH, GI=GI: e.tensor_tensor(out=HH, in0=HH, in1=GI, op=ALU.mult), reads=[K("hh"), K("gin")], writes=[K("hh")])
                PT = PA; ptk = pak

                def tr(e, PT=PT, HH=HH):
                    for q in range(4):
                        ins = e.transpose(out=PT[:, q * 128:(q + 1) * 128], in_=HH[:, q * 128:(q + 1) * 128], identity=identf)
                    return ins
                p.op("pe", tr, reads=[K("hh"), "identf"], writes=[ptk])
                evac(OT[:, :, ct * 128:(ct + 1) * 128], PT.rearrange("p (q n) -> p q n", q=4), [ptk], [otk])
                it += 1
                if ct == 3:
                    p.dma("sp", o_out[Tq * 512:(Tq + 1) * 512, 512:1024].rearrange("(q p) n -> p q n", p=128), OT, reads=[otk])
                l_lists.append(p.end_capture())
        p.replay(l_lists, 4)
        p.barrier()

    if "M" in phases:
        memT_d = nI("memT", [128, 16 * 256])
        wmk_d = nI("wmk", [128, 16 * 512]); wmv_d = nI("wmv", [128, 16 * 512])
        ar.reset()
        memT = ar.alloc([16, 256], BF16); wmk = ar.alloc([16, 512], BF16); wmv = ar.alloc([16, 512], BF16)
        p.dma("pool", memT, memT_d.rearrange("p (k n) -> p k n", k=16), writes=["memT"])
        p.dma("pool", wmk, wmk_d.rearrange("p (k n) -> p k n", k=16), writes=["wmk"])
        p.dma("pool", wmv, wmv_d.rearrange("p (k n) -> p k n", k=16), writes=["wmv"])
        KT = ar.alloc([4, 256], BF16)
        Va = ar.alloc([2, 2, 257], BF16)
        p.op("pool", lambda e: e.memset(Va[:, :, :, 256:257], 1.0), writes=["Va"])
        for tl in range(4):
            P_ = ps[tl % 2]; pk = "PS%d" % (tl % 2)

            def mm(e, P_=P_, tl=tl):
                for kc in range(16):
                    ins = e.matmul(P_[:, 0:256], lhsT=wmk[:, kc, tl * 128:(tl + 1) * 128], rhs=memT[:, kc, :], start=(kc == 0), stop=(kc == 15))
                return ins
            p.op("pe", mm, reads=["wmk", "memT"], writes=[pk])
            evac(KT[:, tl, :], P_[:, 0:256], [pk], ["KT"])
        for mt in range(2):
            P_ = ps[mt % 2]; pk = "PS%d" % (mt % 2)

            def mm(e, P_=P_, mt=mt):
                for kc in range(16):
                    ins = e.matmul(P_, lhsT=memT[:, kc, mt * 128:(mt + 1) * 128], rhs=wmv[:, kc, :], start=(kc == 0), stop=(kc == 15))
                return ins
            p.op("pe", mm, reads=["wmv", "memT"], writes=[pk])
            evac(Va[:, mt, :, 0:256], P_.rearrange("p (h d) -> p h d", h=2), [pk], ["Va"])
        import os
        MSTOP = int(os.environ.get('MSTOP', '9'))
        NB = 2
        qT = [ar.alloc([4, 512], BF16) for _ in range(NB)]
        gm = [ar.alloc([4, 512], F32) for _ in range(NB)]
        PT_ = [ar.alloc([2, 512], BF16) for _ in range(2)]
        rz = [ar.alloc([1], F32) for _ in range(4)]
        om = [ar.alloc([4, 512], F32) for _ in range(NB)]
        omb = [ar.alloc([4, 512], BF16) for _ in range(NB)]
        it = 0
        for Tq in range(S // 512 if MSTOP > 1 else 0):
            b = Tq % NB
            K = lambda n: "%s%d" % (n, b)
            QT = qT[b]; GM = gm[b]; OM = om[b]; OMB = omb[b]
            p.dma("sp", QT, fmb[8:12, :, Tq * 512:(Tq + 1) * 512].rearrange("t p n -> p t n"), writes=[K("qT")])
            p.dma("sp", GM, tm[Tq * 512:(Tq + 1) * 512, 5 * 512:6 * 512].rearrange("(q p) n -> p q n", p=128), writes=[K("gm")])
            p.op("act", lambda e, GM=GM: e.activation(out=GM, in_=GM, func=AF.Silu), reads=[K("gm")], writes=[K("gm")])
            for hd in range(2 if MSTOP > 2 else 0):
                PTb = PT_[it % 2]; ptk = "PT%d" % (it % 2)
                for mt in range(2):
                    P_ = ps[(2 * it + mt) % 4]; pk = "PS%d" % ((2 * it + mt) % 4)

                    def mm(e, P_=P_, hd=hd, mt=mt, QT=QT):
                        for dc in range(2):
                            ins = e.matmul(P_, lhsT=KT[:, 2 * hd + dc, mt * 128:(mt + 1) * 128], rhs=QT[:, 2 * hd + dc, :], start=(dc == 0), stop=(dc == 1))
                        return ins
                    p.op("pe", mm, reads=["KT", K("qT")], writes=[pk])
                    p.op("act", lambda e, P_=P_, PTb=PTb, mt=mt: e.activation(out=PTb[:, mt, :], in_=P_, func=AF.Exp), reads=[pk], writes=[ptk])
                for qs in range(4 if MSTOP > 3 else 0):
                    PO = ps[4 + (it * 4 + qs) % 4]; pok = "PS%d" % (4 + (it * 4 + qs) % 4)

                    def mm2(e, PO=PO, PTb=PTb, qs=qs, hd=hd):
                        for mt in range(2):
                            ins = e.matmul(PO[:, 0:257], lhsT=PTb[:, mt, qs * 128:(qs + 1) * 128], rhs=Va[:, mt, hd, :], start=(mt == 0), stop=(mt == 1))
                        return ins
                    p.op("pe", mm2, reads=[ptk, "Va"], writes=[pok])
                    RZ = rz[(it * 4 + qs) % 4]; rzk = "rz%d" % ((it * 4 + qs) % 4)
                    p.op("dve", lambda e, RZ=RZ, PO=PO: e.reciprocal(out=RZ, in_=PO[:, 256:257]), reads=[pok], writes=[rzk])
                    p.op("dve", lambda e, RZ=RZ, PO=PO, OM=OM, GM=GM, qs=qs, hd=hd: e.scalar_tensor_tensor(
                        out=OM[:, qs, hd * 256:(hd + 1) * 256], in0=PO[:, 0:256], scalar=RZ, in1=GM[:, qs, hd * 256:(hd + 1) * 256],
                        op0=ALU.mult, op1=ALU.mult), reads=[pok, rzk, K("gm")], writes=[K("om")])
                it += 1
            p.op("pool", lambda e, OM=OM, OMB=OMB: e.tensor_copy(out=OMB, in_=OM), reads=[K("om")], writes=[K("omb")])
            p.dma("sp", o_out[Tq * 512:(Tq + 1) * 512, 1536:2048].rearrange("(q p) n -> p q n", p=128), OMB, reads=[K("omb")])
        p.barrier()
    if "N" in phases:
        build_nsa(nc, p, ar, ps, o_out, fmb, tm, sfx, after_q=after_q)
    return o_out


S = 4096
WSEL = 1920
WWIN = 1152
WCMP = 3392
NEG = -30000.0


def build_nsa(nc, p, ar, ps, o_out, fmb, tm, sfx="", after_q=None):
    nI = lambda n, s, d=F32: nc.dram_tensor(n + sfx, s, d, kind="ExternalInput").ap()
    w1k_d = nI("w1k", [128, 32 * 256]); w1v_d = nI("w1v", [128, 32 * 256])
    pek_d = nI("pek", [128, 32]); pev_d = nI("pev", [128, 32])
    w2kp_d = nI("w2kp", [128, 2 * 2 * 128]); w2v_d = nI("w2v", [128, 2 * 64])
    gateb_d = nI("gateb", [128, 24]); ovl_d = nI("ovl", [128, 2 * 64])
    hsel_d = nI("hsel", [8, 128, WSEL]); hwin_d = nI("hwin", [8, 128, WWIN]); hcmp_d = nI("hcmp", [8, 128, WCMP])
    addc_d = nI("addc", [128, 32 * 64]); emat_d = nI("emat", [64, S])
    b31_d = nI("b31", [128, 8])
    identb_d = nI("identn", [128, 128]); identf_d = nI("identnf", [128, 128])
    kcc_s = nc.dram_tensor("kcc_s" + sfx, [128, 256], BF16).ap()
    vc_s = nc.dram_tensor("vc_s" + sfx, [128, 2 * 2 * 129], BF16).ap()
    PK = lambda i: "PS%d" % i
    cnt = {"ev": 0}

    def evac(out, in_, reads, writes):
        cnt["ev"] += 1
        if cnt["ev"] % 2:
            p.op("act", lambda e: e.copy(out=out, in_=in_), reads=reads, writes=writes)
        else:
            p.op("dve", lambda e: e.tensor_copy(out=out, in_=in_), reads=reads, writes=writes)

    ar.reset()
    w1 = [ar.alloc([32, 256], BF16) for _ in range(2)]
    pe = [ar.alloc([32], BF16) for _ in range(2)]
    p.dma("pool", w1[0], w1k_d.rearrange("p (t h) -> p t h", t=32), writes=["w1k"])
    p.dma("pool", w1[1], w1v_d.rearrange("p (t h) -> p t h", t=32), writes=["w1v"])
    p.dma("pool", pe[0], pek_d, writes=["pek"]); p.dma("pool", pe[1], pev_d, writes=["pev"])
    w2kp = ar.alloc([2, 2, 128], BF16); w2v = ar.alloc([2, 64], BF16); ovl = ar.alloc([2, 64], F32)
    p.dma("pool", w2kp, w2kp_d.rearrange("p (a b c) -> p a b c", a=2, b=2), writes=["w2kp"])
    p.dma("pool", w2v, w2v_d.rearrange("p (a c) -> p a c", a=2), writes=["w2v"])
    p.dma("sp", ovl, ovl_d.rearrange("p (a c) -> p a c", a=2), writes=["ovl"])
    xcT = [ar.alloc([256, 16], BF16) for _ in range(2)]
    p.dma("sp", xcT[0].rearrange("p a b -> p (a b)"), fmb[4], writes=["kcT"])
    p.dma("sp", xcT[1].rearrange("p a b -> p (a b)"), fmb[5], writes=["vcT"])
    c1 = ar.alloc([2, 2], F32)
    hT = [ar.alloc([2, 2, 256], BF16) for _ in range(2)]
    kcc = ar.alloc([256], BF16)
    VC = ar.alloc([2, 2, 129], BF16)
    p.op("pool", lambda e: e.memset(VC, 0.0), writes=["VC"])
    p.op("pool", lambda e: e.memset(kcc, 0.0), writes=["kcc"])
    for wi, (wk, pk_) in enumerate((("w1k", "pek"), ("w1v", "pev"))):
        for ht in range(2):
            P_ = ps[ht]

            def mm(e, P_=P_, wi=wi, ht=ht):
                for t in range(32):
                    ins = e.matmul(P_[:, 0:1], lhsT=w1[wi][0:64, t, ht * 128:(ht + 1) * 128], rhs=pe[wi][0:64, t:t + 1], start=(t == 0), stop=(t == 31))
                return ins
            p.op("pe", mm, reads=[wk, pk_], writes=[PK(ht)])
            p.op("dve", lambda e, P_=P_, wi=wi, ht=ht: e.tensor_copy(out=c1[:, wi, ht:ht + 1], in_=P_[:, 0:1]), reads=[PK(ht)], writes=["c1"])
    it = 0
    for wi, (wk, xk) in enumerate((("w1k", "kcT"), ("w1v", "vcT"))):
        for kvl in range(2):
            for ht in range(2):
                P_ = ps[2 + it % 2]; pk = PK(2 + it % 2)

                def mm(e, P_=P_, wi=wi, kvl=kvl, ht=ht):
                    for t in range(32):
                        q, r = divmod(t, 16)
                        ins = e.matmul(P_[:, 0:255], lhsT=w1[wi][kvl * 64:(kvl + 1) * 64, t, ht * 128:(ht + 1) * 128],
                                       rhs=xcT[wi][kvl * 64:(kvl + 1) * 64, q:q + 255, r], start=(t == 0), stop=(t == 31))
                    return ins
                p.op("pe", mm, reads=[wk, xk], writes=[pk])
                p.op("act", lambda e, P_=P_, wi=wi, kvl=kvl, ht=ht: e.activation(out=hT[wi][:, kvl, ht, 0:255], in_=P_[:, 0:255], func=AF.Silu,
                                                                               bias=c1[:, wi, ht:ht + 1], scale=1.0), reads=[pk, "c1"], writes=["hT%d" % wi])
                it += 1
    P_ = ps[4]

    def mmk(e):
        i = 0
        for kvl in range(2):
            for hc in range(2):
                ins = e.matmul(P_[:, 0:255], lhsT=w2kp[:, kvl, hc, :], rhs=hT[0][:, kvl, hc, 0:255], start=(i == 0), stop=(i == 3))
                i += 1
        return ins
    p.op("pe", mmk, reads=["w2kp", "hT0"], writes=[PK(4)])
    p.op("act", lambda e: e.copy(out=kcc[:, 0:255], in_=P_[:, 0:255]), reads=[PK(4)], writes=["kcc"])
    for ec in range(2):
        nn = 128 if ec == 0 else 127
        p.op("dve", lambda e, ec=ec, nn=nn: e.memset(VC[0:nn, ec, :, 64:65], 1.0), reads=["VC"], writes=["VC"])
        for kvl in range(2):
            P2 = ps[5 + (ec * 2 + kvl) % 2]; pk = PK(5 + (ec * 2 + kvl) % 2)

            def mmv(e, P2=P2, ec=ec, kvl=kvl, nn=nn):
                for hc in range(2):
                    ins = e.matmul(P2[0:nn, 0:64], lhsT=hT[1][:, kvl, hc, ec * 128:ec * 128 + nn], rhs=w2v[:, hc, :], start=(hc == 0), stop=(hc == 1))
                return ins
            p.op("pe", mmv, reads=["w2v", "hT1"], writes=[pk])
            p.op("dve", lambda e, P2=P2, ec=ec, kvl=kvl, nn=nn: e.tensor_copy(out=VC[0:nn, ec, kvl, 65:129], in_=P2[0:nn, 0:64]), reads=[pk, "VC"], writes=["VC"])
            p.op("dve", lambda e, ec=ec, kvl=kvl, nn=nn: e.tensor_copy(out=VC[0:nn, ec, kvl, 0:64], in_=ovl[0:nn, ec, :]), reads=["ovl", "VC"], writes=["VC"])
    p.dma("sp", kcc_s, kcc, reads=["kcc"])
    p.dma("sp", vc_s, VC.rearrange("p a b c -> p (a b c)"), reads=["VC"])
    p.barrier()

    import os
    NSTOP = int(os.environ.get('NSTOP', '9'))
    if NSTOP == 0:
        return
    ar.reset()
    identb = ar.alloc([128], BF16); identf = ar.alloc([128], F32)
    p.dma("pool", identb, identb_d, writes=["identb"]); p.dma("sp", identf, identf_d, writes=["identf"])
    kccN = ar.alloc([256], BF16); VCN = ar.alloc([2, 2, 129], BF16)
    p.dma("sp", kccN, kcc_s, writes=["kcc"]); p.dma("sp", VCN.rearrange("p a b c -> p (a b c)"), vc_s, writes=["VC"])
    ksT = ar.alloc([S], BF16); kwT = ar.alloc([S], BF16)
    p.dma("sp", ksT, fmb[6], writes=["ksT"]); p.dma("sp", kwT, fmb[7], writes=["kwT"])
    VS = ar.alloc([32, 2, 65], BF16); VW = ar.alloc([32, 2, 65], BF16)
    p.op("pool", lambda e: e.memset(VS[:, :, :, 64:65], 1.0), writes=["VS"])
    p.op("pool", lambda e: e.memset(VW[:, :, :, 64:65], 1.0), writes=["VW"])
    for kvl in range(2):
        p.dma("pool", VS[:, :, kvl, 0:64], tm[:, 3072 + kvl * 64:3072 + kvl * 64 + 64].rearrange("(t p) c -> p t c", p=128), reads=["VS"], writes=["VS"])
        p.dma("pool", VW[:, :, kvl, 0:64], tm[:, 3200 + kvl * 64:3200 + kvl * 64 + 64].rearrange("(t p) c -> p t c", p=128), reads=["VW"], writes=["VW"])
    hsel = ar.alloc([8, WSEL], BF16); hwin = ar.alloc([8, WWIN], BF16)
    for hl in range(8):
        p.dma("pool", hsel[:, hl, :], hsel_d[hl], writes=["hsel"])
        p.dma("pool", hwin[:, hl, :], hwin_d[hl], writes=["hwin"])
    emat = ar.alloc([S], BF16)
    p.op("pool", lambda e: e.memset(emat[64:128, :], 0.0), writes=["emat"])
    p.dma("pool", emat[0:64, :], emat_d, reads=["emat"], writes=["emat"])
    addc = ar.alloc([32, 64], F32)
    p.dma("sp", addc, addc_d.rearrange("p (a c) -> p a c", a=32), writes=["addc"])
    gateb = ar.alloc([24], F32)
    b31 = ar.alloc([8], F32)
    p.dma("sp", b31, b31_d, writes=["b31"])
    p.dma("sp", gateb, gateb_d, writes=["gateb"])
    NB = 2
    qT = [ar.alloc([4, 512], BF16) for _ in range(NB)]
    gt = [ar.alloc([4, 24], F32) for _ in range(NB)]
    gc = [ar.alloc([4, 512], F32) for _ in range(NB)]
    hc_ = [ar.alloc([8, 2, 512], BF16) for _ in range(NB)]
    imp = ar.alloc([4, 2, 64], F32)
    selb = ar.alloc([4, 2, 64], F32)
    m8 = ar.alloc([4, 2, 8], F32)
    selT = ar.alloc([2, 512], BF16)
    p.op("pool", lambda e: e.memset(selT, 0.0), writes=["selT"])
    PT = [ar.alloc([512], BF16) for _ in range(4)]
    oacc = [ar.alloc([4, 512], F32) for _ in range(NB)]
    oab = [ar.alloc([4, 512], BF16) for _ in range(NB)]
    rz = [ar.alloc([1], F32) for _ in range(4)]
    rg = [ar.alloc([1], F32) for _ in range(4)]
    rz4 = [ar.alloc([4], F32) for _ in range(2)]
    rg4 = [ar.alloc([4], F32) for _ in range(2)]
    tmp4 = [ar.alloc([4, 64], F32) for _ in range(2)]
    sidx = {"s": 0, "pt": 0, "r": 0, "h": 0}

    def s_bank():
        i = sidx["s"] % 4
        sidx["s"] += 1
        return ps[i], PK(i)

    def pt_buf():
        i = sidx["pt"] % 4
        sidx["pt"] += 1
        return PT[i], "PT%d" % i

    def load_q(Q):
        b = Q % NB
        K = lambda n: "%s%d" % (n, b)
        QT = qT[b]; GT = gt[b]; GC = gc[b]; HC = hc_[b]
        nec = 2 if Q >= 4 else 1
        p.dma("sp", QT, fmb[0:4, :, Q * 512:(Q + 1) * 512].rearrange("t p n -> p t n"), writes=[K("qT")])
        p.dma("sp", GT, tm[Q * 512:(Q + 1) * 512, 3328:3352].rearrange("(q p) n -> p q n", p=128), writes=[K("gt")])
        p.dma("sp", GC, tm[Q * 512:(Q + 1) * 512, 4 * 512:5 * 512].rearrange("(q p) n -> p q n", p=128), writes=[K("gc")])
        for hl in range(8):
            for ec in range(nec):
                c0 = min(512 * Q - 2048 * ec, WCMP - 512)
                p.dma("pool", HC[:, hl, ec, :], hcmp_d[hl, :, c0:c0 + 512], writes=[K("hc")])

        def gadd(e, GT=GT):
            for qs in range(4):
                ins = e.tensor_tensor(out=GT[:, qs, :], in0=GT[:, qs, :], in1=gateb, op=ALU.add)
            return ins
        p.op("dve", gadd, reads=[K("gt"), "gateb"], writes=[K("gt")])
        p.op("act", lambda e, GT=GT: e.activation(out=GT, in_=GT, func=AF.Sigmoid), reads=[K("gt")], writes=[K("gt")])
        p.op("act", lambda e, GC=GC: e.activation(out=GC, in_=GC, func=AF.Silu), reads=[K("gc")], writes=[K("gc")])

    NQ = int(os.environ.get('NQ', '8'))
    for Q in range(NQ if NSTOP > 1 else 0):
        b = Q % NB
        K = lambda n: "%s%d" % (n, b)
        QT = qT[b]; GT = gt[b]; GC = gc[b]; HC = hc_[b]; OA = oacc[b]; OB = oab[b]
        nec = 2 if Q >= 4 else 1
        if Q == 0:
            load_q(0)

        def finish_head(PO_list, br, hl, zcol, ocol, first, Kf=K, GT=GT, OA=OA):
            for qs in range(4):
                PO, pok = PO_list[qs]
                ri = sidx["r"] % 4
                sidx["r"] += 1
                RZ = rz[ri]; RG = rg[ri]; rzk = "rz%d" % ri; rgk = "rg%d" % ri
                p.op("dve", lambda e, RZ=RZ, PO=PO: e.tensor_scalar(out=RZ, in0=PO[:, zcol:zcol + 1], scalar1=1e-30, scalar2=None, op0=ALU.max), reads=[pok], writes=[rzk])
                p.op("dve", lambda e, RZ=RZ: e.reciprocal(out=RZ, in_=RZ), reads=[rzk], writes=[rzk])
                p.op("dve", lambda e, RZ=RZ, RG=RG, qs=qs: e.tensor_tensor(out=RG, in0=RZ, in1=GT[:, qs, br * 8 + hl:br * 8 + hl + 1], op=ALU.mult),
                     reads=[rzk, Kf("gt")], writes=[rgk])
                dst = OA[:, qs, hl * 64:(hl + 1) * 64]
                if first:
                    p.op("dve", lambda e, PO=PO, RG=RG, dst=dst: e.tensor_scalar(out=dst, in0=PO[:, ocol:ocol + 64], scalar1=RG, scalar2=None, op0=ALU.mult),
                         reads=[pok, rgk], writes=[Kf("oacc")])
                else:
                    p.op("dve", lambda e, PO=PO, RG=RG, dst=dst: e.scalar_tensor_tensor(out=dst, in0=PO[:, ocol:ocol + 64], scalar=RG, in1=dst, op0=ALU.mult, op1=ALU.add),
                         reads=[pok, rgk, Kf("oacc")], writes=[Kf("oacc")])
                if br == 0:
                    kvl, g = divmod(hl, 4)
                    idst = imp[:, qs, kvl, :]
                    if g == 0:
                        p.op("dve", lambda e, PO=PO, RZ=RZ, idst=idst: e.tensor_scalar(out=idst, in0=PO[:, 0:64], scalar1=RZ, scalar2=None, op0=ALU.mult),
                             reads=[pok, rzk], writes=["imp"])
                    else:
                        p.op("dve", lambda e, PO=PO, RZ=RZ, idst=idst: e.scalar_tensor_tensor(out=idst, in0=PO[:, 0:64], scalar=RZ, in1=idst, op0=ALU.mult, op1=ALU.add),
                             reads=[pok, rzk, "imp"], writes=["imp"])

        def finish_packed(POb, pok, br, hl, Kf=K, GT=GT, OA=OA):
            i2 = sidx["r"] % 2
            sidx["r"] += 1
            RZ = rz4[i2]; RG = rg4[i2]; TM = tmp4[i2]
            rzk = "rz4%d" % i2; rgk = "rg4%d" % i2; tmk = "tmp4%d" % i2
            pv = POb[:, 0:260].rearrange("p (q c) -> p q c", c=65)
            p.op("dve", lambda e: e.tensor_scalar(out=RZ, in0=pv[:, :, 64], scalar1=1e-30, scalar2=None, op0=ALU.max), reads=[pok], writes=[rzk])
            p.op("dve", lambda e: e.reciprocal(out=RZ, in_=RZ), reads=[rzk], writes=[rzk])
            p.op("dve", lambda e: e.tensor_tensor(out=RG, in0=RZ, in1=GT[:, :, br * 8 + hl], op=ALU.mult), reads=[rzk, Kf("gt")], writes=[rgk])
            p.op("dve", lambda e: e.tensor_tensor(out=TM, in0=pv[:, :, 0:64], in1=RG.unsqueeze(2).broadcast_to([128, 4, 64]), op=ALU.mult),
                 reads=[pok, rgk], writes=[tmk])
            dst = OA[:, :, hl * 64:(hl + 1) * 64]
            p.op("pool", lambda e: e.tensor_tensor(out=dst, in0=dst, in1=TM, op=ALU.add), reads=[tmk, Kf("oacc")], writes=[Kf("oacc")])

        pend = []
        DEPTH_SEL = int(os.environ.get("PDEPTH", "2"))

        def flush(keep=0):
            while len(pend) > keep:
                pend.pop(0)()
        for hl in range(8):
            kvl, g = divmod(hl, 4)
            pr = slice(kvl * 64, (kvl + 1) * 64)
            pts = []
            for ec in range(nec):
                SP, spk = s_bank()
                PTb, ptk = pt_buf()

                def mm(e, SP=SP, ec=ec, hl=hl, g=g, pr=pr, QT=QT, HC=HC):
                    e.matmul(SP, lhsT=kccN[pr, ec * 128:(ec + 1) * 128], rhs=QT[pr, g, :], start=True, stop=False)
                    return e.matmul(SP, lhsT=identb, rhs=HC[:, hl, ec, :], start=False, stop=True)
                p.op("pe", mm, reads=["kcc", K("qT"), K("hc"), "identb"], writes=[spk])
                p.op("act", lambda e, SP=SP, PTb=PTb: e.activation(out=PTb, in_=SP, func=AF.Exp), reads=[spk], writes=[ptk])
                pts.append((PTb, ptk))
            flush()

            def second(hl=hl, kvl=kvl, g=g, pts=pts):
                hb = 4 + 2 * (hl % 2)
                for half in range(2):
                    POb = ps[hb + half]; pok = PK(hb + half)

                    def mm2(e, POb=POb, half=half, kvl=kvl, pts=pts):
                        n_ = 0
                        for q2 in range(2):
                            qs = half * 2 + q2
                            for ec in range(len(pts)):
                                ins = e.matmul(POb[:, q2 * 129:(q2 + 1) * 129], lhsT=pts[ec][0][:, qs * 128:(qs + 1) * 128], rhs=VCN[:, ec, kvl, :],
                                               start=(n_ == 0), stop=(ec == len(pts) - 1))
                                n_ += 1
                        return ins
                    p.op("pe", mm2, reads=[k for (_, k) in pts] + ["VC"], writes=[pok])
                    i2 = sidx["r"] % 2
                    sidx["r"] += 1
                    RZ = rz4[i2][:, 0:2]; RG = rg4[i2][:, 0:2]; TM = tmp4[i2][:, 0:2, :]
                    rzk = "rz4%d" % i2; rgk = "rg4%d" % i2; tmk = "tmp4%d" % i2
                    pv = POb[:, 0:258].rearrange("p (q c) -> p q c", c=129)
                    qsl = slice(half * 2, half * 2 + 2)
                    p.op("dve", lambda e, RZ=RZ, pv=pv: e.tensor_scalar(out=RZ, in0=pv[:, :, 64], scalar1=1e-30, scalar2=None, op0=ALU.max), reads=[pok], writes=[rzk])
                    p.op("dve", lambda e, RZ=RZ: e.reciprocal(out=RZ, in_=RZ), reads=[rzk], writes=[rzk])
                    p.op("dve", lambda e, RZ=RZ, RG=RG, qsl=qsl, hl=hl, GT=GT: e.tensor_tensor(out=RG, in0=RZ, in1=GT[:, qsl, hl], op=ALU.mult), reads=[rzk, K("gt")], writes=[rgk])
                    dst = OA[:, qsl, hl * 64:(hl + 1) * 64]
                    p.op("dve", lambda e, pv=pv, RG=RG, dst=dst: e.tensor_tensor(out=dst, in0=pv[:, :, 65:129], in1=RG.unsqueeze(2).broadcast_to([128, 2, 64]), op=ALU.mult),
                         reads=[pok, rgk], writes=[K("oacc")])
                    idst = imp[:, qsl, kvl, :]
                    if g == 0:
                        p.op("dve", lambda e, pv=pv, RZ=RZ, idst=idst: e.tensor_tensor(out=idst, in0=pv[:, :, 0:64], in1=RZ.unsqueeze(2).broadcast_to([128, 2, 64]), op=ALU.mult),
                             reads=[pok, rzk], writes=["imp"])
                    else:
                        p.op("dve", lambda e, pv=pv, RZ=RZ, TM=TM: e.tensor_tensor(out=TM, in0=pv[:, :, 0:64], in1=RZ.unsqueeze(2).broadcast_to([128, 2, 64]), op=ALU.mult),
                             reads=[pok, rzk], writes=[tmk])
                        p.op("pool", lambda e, TM=TM, idst=idst: e.tensor_tensor(out=idst, in0=idst, in1=TM, op=ALU.add), reads=[tmk, "imp"], writes=["imp"])
            pend.append(second)
        flush()
        if Q + 1 < NQ:
            load_q(Q + 1)
        for kvl in range(2 if NSTOP > 2 else 0):
            for qs in range(4):
                I_ = imp[:, qs, kvl, :]; SB = selb[:, qs, kvl, :]; M8 = m8[:, qs, kvl, :]
                p.op("dve", lambda e, I_=I_, qs=qs, Q=Q: e.tensor_tensor(out=I_, in0=I_, in1=addc[:, 4 * Q + qs, :], op=ALU.add), reads=["imp", "addc"], writes=["imp"])
                p.op("dve", lambda e, I_=I_, M8=M8: e.max(out=M8, in_=I_), reads=["imp"], writes=["m8"])
                p.op("dve", lambda e, I_=I_, M8=M8, SB=SB: e.tensor_scalar(out=SB, in0=I_, scalar1=M8[:, 7:8], scalar2=None, op0=ALU.is_ge), reads=["imp", "m8"], writes=["selb"])
                p.op("dve", lambda e, SB=SB: e.tensor_scalar(out=SB, in0=SB, scalar1=-NEG, scalar2=NEG, op0=ALU.mult, op1=ALU.add), reads=["selb"], writes=["selb"])
                SP, spk = s_bank()
                p.op("pe", lambda e, SP=SP, SB=SB: e.transpose(out=SP[0:64, 0:128], in_=SB, identity=identf), reads=["selb", "identf"], writes=[spk])
                p.op("act", lambda e, SP=SP, kvl=kvl, qs=qs: e.copy(out=selT[0:64, kvl, qs * 128:(qs + 1) * 128], in_=SP[0:64, 0:128]), reads=[spk], writes=["selT"])
        for br in ({4: (1,), 5: (2,)}.get(NSTOP, (2, 1)) if NSTOP > 3 else ()):
            kT = ksT if br == 1 else kwT
            kTk = "ksT" if br == 1 else "kwT"
            Vb = VS if br == 1 else VW
            Vk = "VS" if br == 1 else "VW"
            H = hsel if br == 1 else hwin
            Hk = "hsel" if br == 1 else "hwin"
            t_lo = 0 if br == 1 else max(0, 4 * Q - 2)
            t_hi = 4 * Q + 3
            for hl in range(8):
                kvl, g = divmod(hl, 4)
                pr = slice(kvl * 64, (kvl + 1) * 64)
                hb = 4 + sidx["h"] % 2
                sidx["h"] += 1
                POb = ps[hb]; pobk = PK(hb)
                for t in range(t_lo, t_hi + 1):
                    SP, spk = s_bank()
                    PTb, ptk = pt_buf()
                    o = 512 * Q - 128 * t
                    c0 = (min(o, 1024) if br == 1 else o) + 384

                    far = (br == 1 and o >= 1024)

                    def mm(e, SP=SP, t=t, hl=hl, g=g, pr=pr, QT=QT, c0=c0, kT=kT, H=H, br=br, kvl=kvl, far=far):
                        e.matmul(SP, lhsT=kT[pr, t * 128:(t + 1) * 128], rhs=QT[pr, g, :], start=True, stop=False)
                        if br == 1:
                            ins = e.matmul(SP, lhsT=emat[:, t * 128:(t + 1) * 128], rhs=selT[:, kvl, :], start=False, stop=far)
                            if far:
                                return ins
                        return e.matmul(SP, lhsT=identb, rhs=H[:, hl, c0:c0 + 512], start=False, stop=True)
                    p.op("pe", mm, reads=[kTk, K("qT"), Hk, "identb", "emat", "selT"], writes=[spk])
                    if far:
                        p.op("act", lambda e, SP=SP, PTb=PTb, hl=hl: e.activation(out=PTb, in_=SP, func=AF.Exp, bias=b31[:, hl:hl + 1], scale=1.0),
                             reads=[spk, "b31"], writes=[ptk])
                    else:
                        p.op("act", lambda e, SP=SP, PTb=PTb: e.activation(out=PTb, in_=SP, func=AF.Exp), reads=[spk], writes=[ptk])
                    flush(DEPTH_SEL - 1)
                    qss = []
                    for qs in range(4):
                        u = 4 * Q + qs
                        lo = 0 if br == 1 else max(0, u - 2)
                        if lo <= t <= u:
                            qss.append((qs, t == lo, t == u))

                    def second(PTb=PTb, ptk=ptk, t=t, kvl=kvl, qss=qss, Vb=Vb, Vk=Vk, last=(t == t_hi), first_t=(t == t_lo), POb=POb, pobk=pobk, br=br, hl=hl):
                        def mm2(e):
                            for n_, (qs, st_, sp_) in enumerate(qss):
                                ins = e.matmul(POb[:, qs * 65:(qs + 1) * 65], lhsT=PTb[:, qs * 128:(qs + 1) * 128], rhs=Vb[:, t, kvl, :],
                                               start=(first_t and n_ == 0), stop=sp_)
                            return ins
                        p.op("pe", mm2, reads=[ptk, Vk], writes=[pobk])
                        if last:
                            finish_packed(POb, pobk, br, hl)
                    pend.append(second)
            flush()
        p.op("pool", lambda e, OA=OA, GC=GC, OB=OB: e.tensor_tensor(out=OB, in0=OA, in1=GC, op=ALU.mult), reads=[K("oacc"), K("gc")], writes=[K("oab")])
        p.dma("sp", o_out[Q * 512:(Q + 1) * 512, 1024:1536].rearrange("(q p) n -> p q n", p=128), OB, reads=[K("oab")], writes=["orow%d" % Q])
        if after_q is not None:
            after_q(Q)
    p.barrier()


import numpy as np
D_BR = 1024; KVW = 256
SIZES = (D_BR,)*6 + (KVW,)*6 + (48, D_BR, D_BR, D_BR, 8192)
OFF = np.concatenate([[0], np.cumsum(SIZES)]).astype(int)
(O_U, O_V, O_GA, O_XB, O_GB, O_Q, O_KC, O_VC, O_KS, O_VS, O_KW, O_VW, O_GL, O_GC, O_QM, O_GM, O_MG) = [int(v) for v in OFF[:17]]

def arr_k(w, n):
    K, C = w.shape
    nt = C // n
    a = w.reshape(16, 128, nt, n).transpose(2, 1, 0, 3)
    return np.ascontiguousarray(a).reshape(nt, 128, 16 * n)

def s1_inputs(j, x_b, mem_b, l, W):
    w_in = W["w_in"][l]
    f32 = np.float32
    cols = []
    for g in range(4):
        for kvl in range(2):
            hq = (2 * j + kvl) * 4 + g
            cols.append(np.arange(O_Q + hq * 64, O_Q + hq * 64 + 64))
    for base in (O_KC, O_VC, O_KS, O_KW):
        cols.append(np.arange(base + 2 * j * 64, base + 2 * j * 64 + 128))
    cols.append(np.arange(O_QM + 2 * j * 256, O_QM + 2 * j * 256 + 512))
    cols.append(np.arange(O_XB + j * 512, O_XB + j * 512 + 512))
    cols.append(np.arange(O_GB + j * 512, O_GB + j * 512 + 512))
    fm_cols = np.concatenate(cols)
    assert fm_cols.size == 20 * 128
    wfm = arr_k(w_in[:, fm_cols], 128)
    tmc = [np.arange(O_U + j * 512, O_U + j * 512 + 512),
           np.arange(O_V + j * 512, O_V + j * 512 + 512), np.arange(O_V + (1 - j) * 512, O_V + (1 - j) * 512 + 512),
           np.arange(O_GA + j * 512, O_GA + j * 512 + 512), np.arange(O_GC + j * 512, O_GC + j * 512 + 512),
           np.arange(O_GM + j * 512, O_GM + j * 512 + 512)]
    glc = np.concatenate([np.arange(O_GL + br * 16 + 2 * j * 4, O_GL + br * 16 + 2 * j * 4 + 8) for br in range(3)])
    last = np.concatenate([np.arange(O_VS + 2 * j * 64, O_VS + 2 * j * 64 + 128), np.arange(O_VW + 2 * j * 64, O_VW + 2 * j * 64 + 128), glc])
    wl = np.zeros((2048, 512), f32); wl[:, :280] = w_in[:, last]
    wtm = np.concatenate([arr_k(w_in[:, c], 512) for c in tmc] + [arr_k(wl, 512)], axis=0)
    im = {"x_tok1": (np.ascontiguousarray(x_b) if x_b is not None else None), "wfm": wfm, "wtm": wtm, "ident1": np.eye(128, dtype=f32)}
    sw = W["sgu_w"][l][4 * j:4 * j + 4]
    tri = np.tril(np.ones((128, 128), bool))
    sw = np.where(tri[None], sw, 0).astype(f32)
    im["sgw"] = np.ascontiguousarray(sw.transpose(2, 0, 1)).reshape(128, 512)
    im["sgb"] = np.ascontiguousarray(W["sgu_b"][l][4 * j:4 * j + 4].T)
    im["slg"] = np.broadcast_to(W["sgu_ln_g"][l][j * 512:(j + 1) * 512], (128, 512)).copy()
    im["slb"] = np.broadcast_to(W["sgu_ln_b"][l][j * 512:(j + 1) * 512], (128, 512)).copy()
    ch = lambda v: np.ascontiguousarray(v[j * 512:(j + 1) * 512].reshape(4, 128).T)
    cw = W["conv_w"][l][:, j * 512:(j + 1) * 512].reshape(4, 4, 128)
    im["cw"] = np.ascontiguousarray(cw.transpose(2, 1, 0)).reshape(128, 16)
    im["cb"] = ch(W["conv_b"][l])
    im["lwa"] = np.ascontiguousarray(W["lru_wa"][l][4 * j:4 * j + 4].transpose(1, 0, 2)).reshape(128, 512)
    im["lwx"] = np.ascontiguousarray(W["lru_wx"][l][4 * j:4 * j + 4].transpose(1, 0, 2)).reshape(128, 512)
    im["lba"] = ch(W["lru_ba"][l]); im["lbx"] = ch(W["lru_bx"][l]); im["lam"] = ch(W["lru_lambda"][l])
    im["identf"] = np.eye(128, dtype=f32)
    im["memT"] = arr_k(np.ascontiguousarray(mem_b.T), 256)[0]
    wkv = W["w_mem_kv"][l]
    im["wmk"] = arr_k(wkv[:, 2 * j * 256:2 * j * 256 + 512], 512)[0]
    im["wmv"] = arr_k(wkv[:, 1024 + 2 * j * 256:1024 + 2 * j * 256 + 512], 512)[0]
    return im


import numpy as np
WSEL = 1920; WWIN = 1152; WCMP = 3392; NEG = -30000.0

def rel_bucket_np(dist):
    n = np.maximum(dist, 0)
    nf = np.maximum(n, 1).astype(np.float32)
    large = 16 + (np.log(nf / np.float32(16)) / np.float32(np.log(64.0)) * np.float32(16)).astype(np.int32)
    return np.where(n < 16, n, np.minimum(large, 31))

def nsa_inputs(j, l, W):
    f32 = np.float32
    im = {}
    for nm, key in (("w1k", "cmp_w1_k"), ("w1v", "cmp_w1_v")):
        w1 = W[key][l].reshape(32, 64, 256)
        a = np.concatenate([w1, w1], axis=1).transpose(1, 0, 2)
        im[nm] = np.ascontiguousarray(a).reshape(128, 32 * 256)
    for nm, key in (("pek", "cmp_pe_k"), ("pev", "cmp_pe_v")):
        pe = W[key][l]
        im[nm] = np.ascontiguousarray(np.concatenate([pe.T, pe.T], axis=0))
    w2k = W["cmp_w2_k"][l].reshape(2, 128, 64)
    w2kp = np.zeros((128, 2, 2, 128), f32)
    for kvl in range(2):
        for hc in range(2):
            w2kp[:, kvl, hc, kvl * 64:(kvl + 1) * 64] = w2k[hc]
    im["w2kp"] = w2kp.reshape(128, 512)
    im["w2v"] = np.ascontiguousarray(W["cmp_w2_v"][l].reshape(2, 128, 64).transpose(1, 0, 2)).reshape(128, 128)
    gb = W["nsa_gate_b"][l]
    idx = np.concatenate([np.arange(br * 16 + 2 * j * 4, br * 16 + 2 * j * 4 + 8) for br in range(3)])
    im["gateb"] = np.broadcast_to(gb[idx], (128, 24)).copy()
    c_start = np.arange(255) * 16; s_start = np.arange(64) * 64
    ov = np.clip(np.minimum(c_start[:, None] + 32, s_start[None, :] + 64) - np.maximum(c_start[:, None], s_start[None, :]), 0, None).astype(f32) / 32
    ovp = np.zeros((256, 64), f32); ovp[:255] = ov
    im["ovl"] = np.ascontiguousarray(ovp.reshape(2, 128, 64).transpose(1, 0, 2)).reshape(128, 128)
    rb = W["rel_bias"]
    kk = np.arange(128)[:, None]
    d_sel = np.arange(WSEL)[None, :] - 384 - kk
    d_win = np.arange(WWIN)[None, :] - 384 - kk
    d_cmp = np.arange(WCMP)[None, :] - 16 * kk - 31
    b_sel = rel_bucket_np(d_sel); b_win = rel_bucket_np(d_win); b_cmp = rel_bucket_np(d_cmp)
    hsel = np.zeros((8, 128, WSEL), f32); hwin = np.zeros((8, 128, WWIN), f32); hcmp = np.zeros((8, 128, WCMP), f32)
    for hl in range(8):
        kvl, g = divmod(hl, 4)
        hq = (2 * j + kvl) * 4 + g
        col = rb[:, hq]
        hsel[hl] = np.where(d_sel >= 0, col[b_sel], f32(NEG))
        hwin[hl] = np.where((d_win >= 0) & (d_win < 256), col[b_win], f32(NEG))
        hcmp[hl] = np.where(d_cmp >= 0, col[b_cmp], f32(NEG))
    im["hsel"] = hsel; im["hwin"] = hwin; im["hcmp"] = hcmp
    hqs = [(2 * j + hl // 4) * 4 + hl % 4 for hl in range(8)]
    im["b31"] = np.broadcast_to(rb[31, hqs], (128, 8)).copy()
    pos = np.arange(4096); qb = pos // 64; jj = np.arange(64)
    forced = (jj[None, :] == 0) | (jj[None, :] == qb[:, None]) | (jj[None, :] == qb[:, None] - 1)
    future = jj[None, :] > qb[:, None]
    addc = np.where(future, f32(-1e4), np.where(forced, f32(1e4), f32(0))).astype(f32)
    im["addc"] = np.ascontiguousarray(addc.reshape(32, 128, 64).transpose(1, 0, 2)).reshape(128, 32 * 64)
    im["emat"] = (np.arange(4096)[None, :] // 64 == np.arange(64)[:, None]).astype(f32)
    im["identn"] = np.eye(128, dtype=f32); im["identnf"] = np.eye(128, dtype=f32)
    return im


from concourse.bass_utils import run_bass_kernel_spmd

DEPTH = 2
BATCH = 4
PAIRS = [[0, 1], [2, 3], [4, 5], [6, 7]]


def lay_s2_weights(w_gate, w_branch, w_out):
    wgA = w_gate.reshape(16, 128, 4, 16, 128)
    wgA = np.ascontiguousarray(wgA.transpose(3, 2, 1, 0, 4)).reshape(64, 128, 2048)
    wbA = w_branch.reshape(4, 8, 128, 16, 128)
    wbA = np.ascontiguousarray(wbA.transpose(3, 0, 2, 1, 4)).reshape(64, 128, 1024)
    woA = w_out.reshape(16, 128, 4, 512)
    woA = np.ascontiguousarray(woA.transpose(2, 1, 0, 3)).reshape(4, 128, 8192)
    return wgA, wbA, woA


def build_fused():
    nc = bass.Bass("TRN2", target_bir_lowering=False)
    p = Prog(nc)
    ar = Arena(p, "arena", 50000)
    ps = [p.psum("ps%d" % i, [128, 512])[:] for i in range(8)]
    x_in = nc.dram_tensor("x_full", [S, D], F32, kind="ExternalInput").ap()
    y_out = nc.dram_tensor("y_out", [2048, D], F32, kind="ExternalOutput").ap()
    o_loc = [nc.dram_tensor("o_loc%d" % l, [S, 2048], BF16).ap() for l in range(DEPTH)]
    o_all = [nc.dram_tensor("o_all%d" % l, [2 * S, 2048], BF16).ap() for l in range(DEPTH)]
    y0 = nc.dram_tensor("y0", [2048, D], F32).ap()
    xb_loc = nc.dram_tensor("xb_loc", [2048, D], BF16).ap()
    xg = nc.dram_tensor("xg", [S, D], BF16).ap()
    x_loc = nc.dram_tensor("x_loc", [2048, D], F32).ap()
    o_mine = [nc.dram_tensor("o_mine%d" % l, [2, 2048, 2048], BF16).ap() for l in range(DEPTH)]
    p.dma("sp", x_loc.rearrange("(a b) n -> a (b n)", a=128),
          lambda eng: x_in[bass.ds(p.par(eng) * 2048, 2048), :].rearrange("(a b) n -> a (b n)", a=128))
    for l in range(int(os.environ.get("FUSE_L", DEPTH))):
        sfx = "_%d" % l
        if os.environ.get("FUSE_NOS1"):
            nc.dram_tensor("dummy_in" + sfx, [128, 8], F32, kind="ExternalInput")
        else:
          def after_q(Q, l=l):
              if Q >= 1:
                  k = Q - 1
                  p.collective("AllGather", PAIRS, o_loc[l][k * 512:(k + 1) * 512, :], o_all[l][k * 1024:(k + 1) * 1024, :], reads=["orow%d" % k])
          build_s1(nc, p, ar, ps, sfx=sfx, after_q=after_q, x_src=(x_in if l == 0 else xg), o_dst=o_loc[l], x_tile_row=((lambda t: t * 128) if l == 0 else (lambda t: (((t % 16) // 4) * 2 + t // 16) * 512 + (t % 4) * 128)))
        for k in range(7, 8):
            p.collective("AllGather", PAIRS, o_loc[l][k * 512:(k + 1) * 512, :], o_all[l][k * 1024:(k + 1) * 1024, :])
        p.barrier()
        for r in range(2):
            p.dma("sp", o_mine[l][r].rearrange("(k i) n -> k i n", k=4),
                  lambda eng, r=r, l=l: o_all[l].rearrange("(k r i) n -> k r i n", k=8, r=2)[bass.ds(p.par(eng) * 4, 4), r])
        p.barrier()

        def x_rows(r0, l=l):
            if l == 0:
                return x_loc[r0:r0 + 128, :]
            return y0[r0:r0 + 128, :]

        def o_load(lbuf, lk, r0, l=l):
            dst = lbuf.rearrange("p (i r n) -> p i r n", i=4, r=2)
            for r in range(2):
                p.dma("pool", dst[:, :, r, :], o_mine[l][r, r0:r0 + 128, :].rearrange("p (i n) -> p i n", i=4), writes=[lk])

        def y_store(r_, rk, r0, l=l):
            if l == 0:
                p.dma("sp", y0[r0:r0 + 128, :], r_, reads=[rk])
                p.dma("pool", xb_loc[r0:r0 + 128, :], r_, reads=[rk])
            else:
                p.dma("sp", y_out[r0:r0 + 128, :], r_, reads=[rk])
        def after_pass(ps_i, l=l):
            if l == 0:
                for k in (2 * ps_i, 2 * ps_i + 1):
                    p.collective("AllGather", PAIRS, xb_loc[k * 512:(k + 1) * 512, :], xg[k * 1024:(k + 1) * 1024, :])
        if not os.environ.get("FUSE_NOS2"):
            build_s2(nc, p, ar, ps, T2=2048, sfx=sfx, after_pass=after_pass, x_rows=x_rows, o_load=o_load, y_store=y_store)
        if l == 0:
            p.barrier()
    p.emit()
    return nc


def kernel(**inputs):
    W = {k: np.asarray(v) for k, v in inputs.items()}
    x = W["x"].astype(np.float32, copy=False)
    mem = W["mem"]
    ident = np.eye(128, dtype=np.float32)
    nc = build_fused()
    shared = {}
    for l in range(DEPTH):
        wgA, wbA, woA = lay_s2_weights(W["w_in"][l][:, O_MG:O_MG + 8192], W["w_branch"][l], W["w_out"][l])
        s2 = {"wg": wgA, "wb": wbA, "wo": woA, "lng": np.broadcast_to(W["ln_g"][l], (128, 2048)).copy(),
              "lnb": np.broadcast_to(W["ln_b"][l], (128, 2048)).copy(), "ident": ident}
        for j in range(2):
            d = nsa_inputs(j, l, W)
            shared[(l, j)] = (d, s2)
    in_maps = []
    for c in range(8):
        b, j = divmod(c, 2)
        im = {"x_full": np.ascontiguousarray(x[b])}
        for l in range(DEPTH):
            d, s2 = shared[(l, j)]
            s1 = s1_inputs(j, None, mem[b], l, W)
            s1.pop("x_tok1")
            for k, v in list(s1.items()) + list(d.items()) + list(s2.items()):
                im[k + "_%d" % l] = v
        in_maps.append(im)
    res = run_bass_kernel_spmd(nc, in_maps, core_ids=list(range(8)))
    out = np.empty_like(x)
    for c in range(8):
        b, j = divmod(c, 2)
        out[b, j * 2048:(j + 1) * 2048] = np.asarray(res.results[c]["y_out"])
    return out
```
